# Optimizing a Trainium2 kernel written in Bass

```python
import jax
import jax.numpy as jnp
from jax import lax
import numpy as np

D_MODEL = 1024
BATCH = 8
SEQ = 4096
DEPTH = 2

GRID_W = 64
CTX_LEN = 256
N_EVEN = (DEPTH + 1) // 2
N_ODD = DEPTH // 2
NORM_EPS = 1e-6
N_MOD = 6

A_HEADS = 8
A_KV_HEADS = 2
A_GROUP = A_HEADS // A_KV_HEADS
A_HEAD_DIM = 64
A_Q = A_HEADS * A_HEAD_DIM
A_KV = A_KV_HEADS * A_HEAD_DIM
Q_BLOCK = 128
ROPE_THETA = 10000.0

B_HEADS = 4
B_DK = 64
B_DV = 128
B_K = B_HEADS * B_DK
B_V = B_HEADS * B_DV
B_GATE_RANK = 16
B_GATE_TAU = 16.0
B_CHUNK = 64

EVEN_SIZES = (A_Q, A_KV, A_KV, B_K, B_K, B_V, B_V, B_GATE_RANK, B_GATE_RANK)
EVEN_IN = sum(EVEN_SIZES)
EVEN_MIX = A_Q + B_V

LRU_WIDTH = 1280
LRU_HEADS = 10
LRU_HEAD_DIM = LRU_WIDTH // LRU_HEADS
LRU_CONV = 4
LRU_C = 8.0

D_FF = 2816
FFN_CONV = 3

kernel_name = 'hybrid_gqa_gla_rglru_convffn_prefix_dit'


def rmsnorm(x, g):
    xf = x.astype(jnp.float32)
    y = xf * lax.rsqrt(jnp.mean(xf * xf, axis=-1, keepdims=True) + NORM_EPS)
    return (y * g.astype(jnp.float32)).astype(x.dtype)


def adaln_params(cond, w, b):
    m = jax.nn.silu(cond) @ w + b
    return m.reshape(m.shape[:-1] + (N_MOD, D_MODEL))


def depthwise_conv(x, w):
    k = w.shape[0]
    left = (k - 1) // 2
    return lax.conv_general_dilated(x, w[:, None, :].astype(x.dtype), window_strides=(1,),
                                    padding=[(left, k - 1 - left)],
                                    dimension_numbers=('NWC', 'WIO', 'NWC'),
                                    feature_group_count=x.shape[-1])


def axial_rope_tables(n_tokens):
    rows = n_tokens // GRID_W
    row = jnp.repeat(jnp.arange(rows, dtype=jnp.float32), GRID_W)
    col = jnp.tile(jnp.arange(GRID_W, dtype=jnp.float32), rows)
    n_freq = A_HEAD_DIM // 4
    inv_freq = ROPE_THETA ** (-jnp.arange(n_freq, dtype=jnp.float32) / n_freq)
    ang = jnp.concatenate([row[:, None] * inv_freq, col[:, None] * inv_freq], axis=-1)
    return jnp.cos(ang), jnp.sin(ang)


def apply_axial_rope(x, cos, sin):
    n = A_HEAD_DIM // 4
    xf = x.astype(jnp.float32)
    r1, r2, c1, c2 = jnp.split(xf, 4, axis=-1)
    cr, cc = cos[:, :n], cos[:, n:]
    sr, sc = sin[:, :n], sin[:, n:]
    out = jnp.concatenate([r1 * cr - r2 * sr, r2 * cr + r1 * sr,
                           c1 * cc - c2 * sc, c2 * cc + c1 * sc], axis=-1)
    return out.astype(x.dtype)


def attn_heads(q, k, v, q_gain, k_gain):
    b, l, _ = q.shape
    q = rmsnorm(q.reshape(b, l, A_KV_HEADS, A_GROUP, A_HEAD_DIM), q_gain).transpose(0, 2, 3, 1, 4)
    k = rmsnorm(k.reshape(b, l, A_KV_HEADS, A_HEAD_DIM), k_gain).transpose(0, 2, 1, 3)
    v = v.reshape(b, l, A_KV_HEADS, A_HEAD_DIM).transpose(0, 2, 1, 3)
    return q, k, v


def attend(q, k, v):
    s = jnp.einsum('bkgqd,bksd->bkgqs', q, k).astype(jnp.float32)
    p = jax.nn.softmax(s, axis=-1).astype(v.dtype)
    return jnp.einsum('bkgqs,bksd->bkgqd', p, v)


def gla_chunk_scan(q, k, v, log_a, s0):
    b, h, l, dk = q.shape
    dv = v.shape[-1]
    n = l // B_CHUNK

    def chunks(t):
        return t.reshape(b, h, n, B_CHUNK, t.shape[-1]).transpose(2, 0, 1, 3, 4)

    lower_tri = jnp.tril(jnp.ones((B_CHUNK, B_CHUNK), dtype=bool))[:, :, None]

    def step(state, inp):
        qc, kc, vc, ac = inp
        cum = jnp.cumsum(ac, axis=2)
        o_inter = jnp.einsum('bhtd,bhdv->bhtv', qc * jnp.exp(cum), state)
        diff = jnp.where(lower_tri, cum[:, :, :, None, :] - cum[:, :, None, :, :], -jnp.inf)
        scores = jnp.einsum('bhtd,bhsd,bhtsd->bhts', qc, kc, jnp.exp(diff))
        o_intra = jnp.einsum('bhts,bhsv->bhtv', scores, vc)
        last = cum[:, :, -1:, :]
        state = jnp.exp(last[:, :, 0, :, None]) * state + jnp.einsum('bhsd,bhsv->bhdv', kc * jnp.exp(last - cum), vc)
        return state, o_inter + o_intra

    state, o = lax.scan(step, s0, (chunks(q), chunks(k), chunks(v), chunks(log_a)))
    return o.transpose(1, 2, 0, 3, 4).reshape(b, h, l, dv), state


def flip_seq(t):
    return jnp.flip(t, axis=2)


def gla_bidir(q, k, v, la_f, la_b, s0_f, s0_b):
    o_f, s_f = gla_chunk_scan(q, k, v, la_f, s0_f)
    o_b, s_b = gla_chunk_scan(flip_seq(q), flip_seq(k), flip_seq(v), flip_seq(la_b), s0_b)
    return o_f + flip_seq(o_b), s_f, s_b


def gla_prep(qb, kb, vb, lrf, lrb, w_up, b_gate):
    b, l, _ = qb.shape

    def heads(t, d):
        return t.reshape(b, l, B_HEADS, d).transpose(0, 2, 1, 3).astype(jnp.float32)

    def log_gate(lr, d):
        z = (lr @ w_up[d] + b_gate[d]).astype(jnp.float32)
        return heads(jax.nn.log_sigmoid(z) / B_GATE_TAU, B_DK)

    return (heads(qb, B_DK) * B_DK ** -0.5, heads(kb, B_DK), heads(vb, B_DV),
            log_gate(lrf, 0), log_gate(lrb, 1))


def gla_out(o, gate, o_gain):
    b, h, l, dv = o.shape
    o = rmsnorm(o.transpose(0, 2, 1, 3), o_gain)
    g = gate.reshape(b, l, h, dv).astype(jnp.float32)
    return (o * jax.nn.silu(g)).reshape(b, l, h * dv)


def even_mixer(h_lat, h_ctx, rope, w_in, w_out, q_gain, k_gain, gate_w_up, gate_b, o_gain, need_ctx_out):
    split_at = np.cumsum(EVEN_SIZES)[:-1].tolist()
    qa, ka, va, qb, kb, vb, gb, lrf, lrb = jnp.split(h_lat @ w_in, split_at, axis=-1)
    cqa, cka, cva, cqb, ckb, cvb, cgb, clrf, clrb = jnp.split(h_ctx @ w_in, split_at, axis=-1)
    cos, sin = rope
    dt = h_lat.dtype
    q, k, v = attn_heads(qa, ka, va, q_gain, k_gain)
    q = apply_axial_rope(q, cos, sin) * A_HEAD_DIM ** -0.5
    k = apply_axial_rope(k, cos, sin)
    cq, ck, cv = attn_heads(cqa, cka, cva, q_gain, k_gain)
    k_all = jnp.concatenate([ck, k], axis=2)
    v_all = jnp.concatenate([cv, v], axis=2)
    b, hk, g, l, hd = q.shape
    nb = l // Q_BLOCK
    q_blocks = q.reshape(b, hk, g, nb, Q_BLOCK, hd).transpose(3, 0, 1, 2, 4, 5)
    a_lat = lax.map(lambda qi: attend(qi, k_all, v_all), q_blocks)
    a_lat = a_lat.transpose(1, 0, 4, 2, 3, 5).reshape(b, l, A_Q)
    q2, k2, v2, laf, lab = gla_prep(qb, kb, vb, lrf, lrb, gate_w_up, gate_b)
    cq2, ck2, cv2, claf, clab = gla_prep(cqb, ckb, cvb, clrf, clrb, gate_w_up, gate_b)
    zeros = jnp.zeros((b, B_HEADS, B_DK, B_DV), jnp.float32)
    co2, cs_f, cs_b = gla_bidir(cq2, ck2, cv2, claf, clab, zeros, zeros)
    o2, _, _ = gla_bidir(q2, k2, v2, laf, lab, cs_f, cs_b)
    y_lat = jnp.concatenate([a_lat.astype(dt), gla_out(o2, gb, o_gain).astype(dt)], axis=-1) @ w_out
    if not need_ctx_out:
        return y_lat, None
    a_ctx = attend(cq * A_HEAD_DIM ** -0.5, ck, cv).transpose(0, 3, 1, 2, 4).reshape(b, -1, A_Q)
    y_ctx = jnp.concatenate([a_ctx.astype(dt), gla_out(co2, cgb, o_gain).astype(dt)], axis=-1) @ w_out
    return y_lat, y_ctx


def rglru_inputs(u, lam, w_a, b_a, w_x, b_x):
    b, l, w = u.shape
    uf = u.astype(jnp.float32)
    ub = uf.reshape(b, l, LRU_HEADS, LRU_HEAD_DIM)
    r = jax.nn.sigmoid(jnp.einsum('blhi,hij->blhj', ub, w_a.astype(jnp.float32)).reshape(b, l, w) + b_a.astype(jnp.float32))
    i = jax.nn.sigmoid(jnp.einsum('blhi,hij->blhj', ub, w_x.astype(jnp.float32)).reshape(b, l, w) + b_x.astype(jnp.float32))
    log_a = -LRU_C * r * jax.nn.softplus(-lam.astype(jnp.float32))
    a = jnp.exp(log_a)
    mult = jnp.sqrt(-jnp.expm1(2.0 * log_a))
    return a, mult * i * uf


def linear_scan(a, x, h0, reverse):
    def combine(e1, e2):
        a1, x1 = e1
        a2, x2 = e2
        return a1 * a2, a2 * x1 + x2
    a_cum, x_cum = lax.associative_scan(combine, (a, x), reverse=reverse, axis=1)
    return a_cum * h0[:, None, :] + x_cum


def odd_mixer(h_lat, h_ctx, w_in, conv_w, lam, w_a, b_a, w_x, b_x, w_out, need_ctx_out):
    w_gate, w_rec = w_in[:, :LRU_WIDTH], w_in[:, LRU_WIDTH:]

    def dir_inputs(u, d):
        return rglru_inputs(u, lam[d], w_a[d], b_a[d], w_x[d], b_x[d])

    u_c = depthwise_conv(h_ctx @ w_rec, conv_w)
    u_l = depthwise_conv(h_lat @ w_rec, conv_w)
    h0 = jnp.zeros((h_ctx.shape[0], LRU_WIDTH), jnp.float32)
    hf_c = linear_scan(*dir_inputs(u_c, 0), h0, reverse=False)
    hb_c = linear_scan(*dir_inputs(u_c, 1), h0, reverse=True)
    hf_l = linear_scan(*dir_inputs(u_l, 0), hf_c[:, -1], reverse=False)
    hb_l = linear_scan(*dir_inputs(u_l, 1), hb_c[:, 0], reverse=True)

    def out(h, rec):
        return (jax.nn.gelu((h @ w_gate).astype(jnp.float32)) * rec).astype(h.dtype) @ w_out

    y_lat = out(h_lat, hf_l + hb_l)
    if not need_ctx_out:
        return y_lat, None
    return y_lat, out(h_ctx, hf_c + hb_c)


def conv_ffn(h, w_up, conv_w, w_down):
    u = depthwise_conv(h @ w_up, conv_w)
    gate, val = jnp.split(u, 2, axis=-1)
    return (jax.nn.silu(gate) * val) @ w_down


def setup_inputs(seed: int = 0) -> dict:
    key = jax.random.key(seed)
    ks = list(jax.random.split(key, 32))

    def nrm(i, shape, scale):
        return jax.random.normal(ks[i], shape, jnp.float32) * scale

    D = D_MODEL
    u = jax.random.uniform(ks[20], (N_ODD, 2, LRU_WIDTH), jnp.float32, minval=0.9, maxval=0.999)
    p = u ** (1.0 / LRU_C)
    lam = jnp.log(p) - jnp.log1p(-p)
    return {
        'x': nrm(0, (BATCH, SEQ, D), 1.0),
        'c': nrm(1, (BATCH, D), 1.0),
        'ctx': nrm(2, (BATCH, CTX_LEN, D), 1.0),
        'c_ctx': nrm(3, (D,), 1.0),
        'ada_w': nrm(4, (DEPTH, D, N_MOD * D), 0.5 * D ** -0.5),
        'ada_b': nrm(5, (DEPTH, N_MOD * D), 0.01),
        'norm_mix': 1.0 + nrm(6, (DEPTH, D), 0.02),
        'norm_ffn': 1.0 + nrm(7, (DEPTH, D), 0.02),
        'ffn_w_up': nrm(8, (DEPTH, D, 2 * D_FF), D ** -0.5),
        'ffn_conv': nrm(9, (DEPTH, FFN_CONV, 2 * D_FF), FFN_CONV ** -0.5),
        'ffn_w_down': nrm(10, (DEPTH, D_FF, D), D_FF ** -0.5),
        'even_w_in': nrm(11, (N_EVEN, D, EVEN_IN), D ** -0.5),
        'even_w_out': nrm(12, (N_EVEN, EVEN_MIX, D), EVEN_MIX ** -0.5),
        'attn_q_gain': 1.0 + nrm(13, (N_EVEN, A_HEAD_DIM), 0.02),
        'attn_k_gain': 1.0 + nrm(14, (N_EVEN, A_HEAD_DIM), 0.02),
        'gla_gate_w_up': nrm(15, (N_EVEN, 2, B_GATE_RANK, B_K), B_GATE_RANK ** -0.5),
        'gla_gate_b': nrm(16, (N_EVEN, 2, B_K), 0.5),
        'gla_out_gain': 1.0 + nrm(17, (N_EVEN, B_DV), 0.02),
        'lru_w_in': nrm(18, (N_ODD, D, 2 * LRU_WIDTH), D ** -0.5),
        'lru_conv': nrm(19, (N_ODD, LRU_CONV, LRU_WIDTH), LRU_CONV ** -0.5),
        'lru_lambda': lam,
        'lru_w_a': nrm(21, (N_ODD, 2, LRU_HEADS, LRU_HEAD_DIM, LRU_HEAD_DIM), LRU_HEAD_DIM ** -0.5),
        'lru_b_a': nrm(22, (N_ODD, 2, LRU_WIDTH), 0.1),
        'lru_w_x': nrm(23, (N_ODD, 2, LRU_HEADS, LRU_HEAD_DIM, LRU_HEAD_DIM), LRU_HEAD_DIM ** -0.5),
        'lru_b_x': nrm(24, (N_ODD, 2, LRU_WIDTH), 0.1),
        'lru_w_out': nrm(25, (N_ODD, LRU_WIDTH, D), LRU_WIDTH ** -0.5),
        'final_gain': 1.0 + nrm(26, (D,), 0.02),
    }


def reference(x, c, ctx, c_ctx, ada_w, ada_b, norm_mix, norm_ffn, ffn_w_up, ffn_conv, ffn_w_down,
              even_w_in, even_w_out, attn_q_gain, attn_k_gain, gla_gate_w_up, gla_gate_b, gla_out_gain,
              lru_w_in, lru_conv, lru_lambda, lru_w_a, lru_b_a, lru_w_x, lru_b_x, lru_w_out, final_gain):
    rope = axial_rope_tables(x.shape[1])
    for l in range(DEPTH):
        last = l == DEPTH - 1
        j = l // 2
        m = adaln_params(c, ada_w[l], ada_b[l])[:, None]
        mc = adaln_params(c_ctx, ada_w[l], ada_b[l])
        h_lat = rmsnorm(x, norm_mix[l]) * (1.0 + m[:, :, 1]) + m[:, :, 0]
        h_ctx = rmsnorm(ctx, norm_mix[l]) * (1.0 + mc[1]) + mc[0]
        if l % 2 == 0:
            y_lat, y_ctx = even_mixer(h_lat, h_ctx, rope, even_w_in[j], even_w_out[j], attn_q_gain[j],
                                      attn_k_gain[j], gla_gate_w_up[j], gla_gate_b[j], gla_out_gain[j],
                                      not last)
        else:
            y_lat, y_ctx = odd_mixer(h_lat, h_ctx, lru_w_in[j], lru_conv[j], lru_lambda[j], lru_w_a[j],
                                     lru_b_a[j], lru_w_x[j], lru_b_x[j], lru_w_out[j], not last)
        x = x + m[:, :, 2] * y_lat
        h = rmsnorm(x, norm_ffn[l]) * (1.0 + m[:, :, 4]) + m[:, :, 3]
        x = x + m[:, :, 5] * conv_ffn(h, ffn_w_up[l], ffn_conv[l], ffn_w_down[l])
        if not last:
            ctx = ctx + mc[2] * y_ctx
            hc = rmsnorm(ctx, norm_ffn[l]) * (1.0 + mc[4]) + mc[3]
            ctx = ctx + mc[5] * conv_ffn(hc, ffn_w_up[l], ffn_conv[l], ffn_w_down[l])
    return rmsnorm(x, final_gain)
```

```python
import os
import contextlib
import numpy as np
import ml_dtypes
import concourse.bass as bass
import concourse.mybir as mybir
from concourse.bass_utils import run_bass_kernel_spmd

F32 = mybir.dt.float32
BF16 = mybir.dt.bfloat16
AF = mybir.ActivationFunctionType
ALU = mybir.AluOpType
AX = mybir.AxisListType

D = 1024
T = 4096
C = 256
TOK = T + C
NT = TOK // 128
EPS = 1e-6
DFF = 2816
HW = 4868
LATC = 259
DEBUG = os.environ.get("MK_DEBUG", "") != ""
STOP_AFTER = os.environ.get("MK_STOP", "")


class Buf:
    __slots__ = ("t", "w", "r", "name")

    def __init__(self, t, name=""):
        self.t = t
        self.w = None
        self.r = {}
        self.name = name

    def __getitem__(self, key):
        return self.t[key]


class Eng:
    def __init__(self, k, name, raw, skip_own=False):
        self.k = k
        self.name = name
        self.raw = raw
        self.skip_own = skip_own
        self.sem = k.new_sem("e_" + name)
        self.cnt = 0
        self.seen = {}

    def wait_tok(self, tok):
        sem, val = tok
        if self.seen.get(id(sem), 0) >= val:
            return
        self.raw.wait_ge(sem, val)
        self.seen[id(sem)] = val


class K:
    def __init__(self, nc, ndma=12):
        self.nc = nc
        self.es = contextlib.ExitStack()
        self.pe = Eng(self, "pe", nc.tensor, skip_own=True)
        self.act = Eng(self, "act", nc.scalar)
        self.dve = Eng(self, "dve", nc.vector)
        self.pool = Eng(self, "pool", nc.gpsimd)
        self.sp = Eng(self, "sp", nc.sync)
        self.engs = [self.pe, self.act, self.dve, self.pool, self.sp]
        self.dq = {}
        for qn, e in (("sp", self.sp), ("pool", self.pool)):
            self.dq[qn] = dict(eng=e, sems=[[self.new_sem("d_%s%d" % (qn, i)), 0] for i in range(ndma)], idx=0)
        self.n_inst = 0

    def new_sem(self, name):
        return self.es.enter_context(self.nc.semaphore(name))

    def sb(self, stack, name, shape, dtype):
        self.uid = getattr(self, "uid", 0) + 1
        name = "%s_%d" % (name, self.uid)
        return Buf(stack.enter_context(self.nc.sbuf_tensor(name, list(shape), dtype)), name)

    def ps(self, stack, name, shape, dtype=F32):
        self.uid = getattr(self, "uid", 0) + 1
        name = "%s_%d" % (name, self.uid)
        return Buf(stack.enter_context(self.nc.psum_tensor(name, list(shape), dtype)), name)

    def dram(self, name, shape, dtype, kind="Internal"):
        return Buf(self.nc.dram_tensor(name, list(shape), dtype, kind=kind).ap(), name)

    def _deps(self, eng, reads, writes):
        for b in reads:
            if b.w is not None and not (eng.skip_own and b.w[0] is eng.sem):
                eng.wait_tok(b.w)
        for b in writes:
            if b.w is not None and not (eng.skip_own and b.w[0] is eng.sem):
                eng.wait_tok(b.w)
            for tok in b.r.values():
                if not (eng.skip_own and tok[0] is eng.sem):
                    eng.wait_tok(tok)

    def _mark(self, tok, key, reads, writes):
        for b in reads:
            b.r[key] = tok
        for b in writes:
            b.w = tok
            b.r = {}

    def op(self, eng, fn, reads=(), writes=()):
        self._deps(eng, reads, writes)
        inst = fn(eng.raw)
        eng.cnt += 1
        inst.then_inc(eng.sem, 1)
        self._mark((eng.sem, eng.cnt), eng.name, reads, writes)
        self.n_inst += 1

    def dma(self, out, in_, reads=(), writes=(), q="sp", **kw):
        Q = self.dq[q]
        eng = Q["eng"]
        self._deps(eng, reads, writes)
        slot = Q["sems"][Q["idx"] % len(Q["sems"])]
        Q["idx"] += 1
        if slot[1] > 0:
            eng.wait_tok((slot[0], slot[1]))
        slot[1] += 16
        eng.raw.dma_start(out=out, in_=in_, **kw).then_inc(slot[0], 16)
        tok = (slot[0], slot[1])
        self._mark(tok, "dma_%s_%d" % (q, id(slot)), reads, writes)
        self.n_inst += 1

    def barrier(self):
        toks = [(e.sem, e.cnt) for e in self.engs if e.cnt > 0]
        for Q in self.dq.values():
            for s in Q["sems"]:
                if s[1] > 0:
                    toks.append((s[0], s[1]))
        for e in self.engs:
            for t in toks:
                if t[0] is not e.sem:
                    e.wait_tok(t)

    def mm(self, out, lhsT, rhs, start, stop, reads, writes):
        self.op(self.pe, lambda e: e.matmul(out, lhsT=lhsT, rhs=rhs, start=start, stop=stop), reads, writes)

    def tr(self, out, in_, ident, reads, writes):
        self.op(self.pe, lambda e: e.transpose(out, in_, ident), reads, writes)

    def actf(self, out, in_, func, reads, writes, **kw):
        self.op(self.act, lambda e: e.activation(out=out, in_=in_, func=func, **kw), reads, writes)

    def tt(self, eng, out, in0, in1, op, reads, writes):
        self.op(eng, lambda e: e.tensor_tensor(out=out, in0=in0, in1=in1, op=op), reads, writes)

    def ts(self, eng, out, in0, s1, s2, op0, op1, reads, writes):
        if s2 is None:
            self.op(eng, lambda e: e.tensor_scalar(out=out, in0=in0, scalar1=s1, scalar2=None, op0=op0), reads, writes)
        else:
            self.op(eng, lambda e: e.tensor_scalar(out=out, in0=in0, scalar1=s1, scalar2=s2, op0=op0, op1=op1), reads, writes)

    def stt(self, eng, out, in0, scalar, in1, op0, op1, reads, writes):
        self.op(eng, lambda e: e.scalar_tensor_tensor(out=out, in0=in0, scalar=scalar, in1=in1, op0=op0, op1=op1), reads, writes)

    def cp(self, eng, out, in_, reads, writes):
        if eng is self.act:
            self.op(eng, lambda e: e.copy(out=out, in_=in_), reads, writes)
        else:
            self.op(eng, lambda e: e.tensor_copy(out=out, in_=in_), reads, writes)

    def rstd(self, out, tmp, ss, inv_n, reads_ss):
        b_ss, b_tmp, b_out = reads_ss
        self.actf(tmp, ss, AF.Ln, [b_ss], [b_tmp], scale=inv_n, bias=EPS)
        self.actf(out, tmp, AF.Exp, [b_tmp], [b_out], scale=-0.5)


def build_program():
    nc = bass.Bass("TRN2", target_bir_lowering=False)
    k = K(nc)

    def din(name, shape, dt=F32):
        return Buf(nc.dram_tensor(name, list(shape), dt, kind="ExternalInput").ap(), name)

    xc = din("xc", [TOK, D])
    c2T = din("c2T", [128, 8, 2])
    ada_w = din("ada_w", [2, D, 6 * D])
    ada_b = din("ada_b", [2, 6 * D])
    norm_mix = din("norm_mix", [2, D])
    norm_ffn = din("norm_ffn", [2, D])
    ffn_w_up = din("ffn_w_up", [2, D, 2 * DFF])
    ffn_conv = din("ffn_conv", [2, 128, 44, 3])
    ffn_w_down = din("ffn_w_down", [2, DFF, D])
    even_w_in = din("even_w_in", [D, 2336])
    even_w_out = din("even_w_out", [D, D])
    qk_gain = din("qk_gain", [1, 640])
    gate_wup = din("gate_wup", [33, 512])
    gla_og = din("gla_og", [1, 512])
    lru_w_in = din("lru_w_in", [D, 2560])
    lru_conv = din("lru_conv", [128, 10, 4])
    lru_lam = din("lru_lam", [128, 2, 10])
    lru_w_a = din("lru_w_a", [2, 10, 128, 128])
    lru_b_a = din("lru_b_a", [128, 2, 10])
    lru_w_x = din("lru_w_x", [2, 10, 128, 128])
    lru_b_x = din("lru_b_x", [128, 2, 10])
    lru_w_out = din("lru_w_out", [1280, D])
    final_gain = din("final_gain", [1, D])
    c_ident = din("c_ident", [128, 128])
    c_maskf = din("c_maskf", [128, 128])
    c_maskb = din("c_maskb", [128, 128])
    c_cos = din("c_cos", [128, 32, 64])
    c_sin = din("c_sin", [128, 32, 64])
    out_d = Buf(nc.dram_tensor("out", [T, D], F32, kind="ExternalOutput").ap(), "out")

    dbg = {}

    def scratch(name, shape, dt):
        kind = "ExternalOutput" if DEBUG else "Internal"
        b = Buf(nc.dram_tensor(name, list(shape), dt, kind=kind).ap(), name)
        dbg[name] = b
        return b

    mrows = scratch("mrows", [2, 2, 6 * D], F32)
    qT_d = scratch("qT_d", [4, 128, TOK], BF16)
    kT_d = scratch("kT_d", [2, 128, TOK], BF16)
    vA_d = scratch("vA_d", [TOK, 130], BF16)
    qBT_d = scratch("qBT_d", [2, 128, TOK], BF16)
    kBT_d = scratch("kBT_d", [2, 128, TOK], BF16)
    vB_d = scratch("vB_d", [TOK, 512], BF16)
    gB_d = scratch("gB_d", [TOK, 512], BF16)
    sp_d = scratch("sp_d", [TOK, 512], F32)
    oF_d = scratch("oF_d", [TOK, 512], F32)
    mixAT_d = scratch("mixAT_d", [8, 64, TOK], BF16)
    mixBT_d = scratch("mixBT_d", [4, 128, TOK], BF16)
    x1_d = scratch("x1_d", [TOK, D], F32)
    h2T_d = scratch("h2T_d", [8, 128, HW], BF16)
    xc2_d = scratch("xc2_d", [TOK, D], F32)
    rgT_d = scratch("rgT_d", [10, 128, T], BF16)

    pe, act, dve, pool = k.pe, k.act, k.dve, k.pool

    with k.es, contextlib.ExitStack() as cst:
        identf = k.sb(cst, "identf", [128, 128], F32)
        identb = k.sb(cst, "identb", [128, 128], BF16)
        maskf = k.sb(cst, "maskf", [128, 128], F32)
        maskb = k.sb(cst, "maskb", [128, 128], F32)
        trif = k.sb(cst, "trif", [128, 128], F32)
        trib = k.sb(cst, "trib", [128, 128], F32)
        ones = k.sb(cst, "ones", [128, 128], F32)
        k.dma(identf[:], c_ident[:, :], writes=[identf])
        k.dma(maskf[:], c_maskf[:, :], writes=[maskf])
        k.dma(maskb[:], c_maskb[:, :], writes=[maskb])
        k.cp(dve, identb[:], identf[:], [identf], [identb])
        k.ts(dve, trif[:], maskf[:], -1.0 / 16.0, None, ALU.mult, ALU.bypass, [maskf], [trif])
        k.ts(dve, trib[:], maskb[:], -1.0 / 16.0, None, ALU.mult, ALU.bypass, [maskb], [trib])
        k.op(dve, lambda e: e.memset(ones[:], 1.0), [], [ones])

        def bcast_load(dst, src_row_ap):
            k.dma(dst[:], src_row_ap.partition_broadcast(128), writes=[dst])

        def mrow(l, r, j):
            return mrows[l, r:r + 1, j * D:(j + 1) * D]

        def load_mod(st, l, which, norm_w):
            res = {}
            nm = k.sb(st, "nm", [128, D], F32)
            bcast_load(nm, norm_w[l:l + 1, :])
            for r in (0, 1):
                sg = k.sb(st, "sg%d" % r, [128, D], F32)
                sh = k.sb(st, "sh%d" % r, [128, D], F32)
                gt = k.sb(st, "gt%d" % r, [128, D], F32)
                k.dma(sg[:], mrow(l, r, 3 * which + 1).partition_broadcast(128), reads=[mrows], writes=[sg])
                k.dma(sh[:], mrow(l, r, 3 * which + 0).partition_broadcast(128), reads=[mrows], writes=[sh])
                k.dma(gt[:], mrow(l, r, 3 * which + 2).partition_broadcast(128), reads=[mrows], writes=[gt])
                k.stt(dve, sg[:], sg[:], 1.0, nm[:], ALU.add, ALU.mult, [sg, nm], [sg])
                res[r] = (sg, sh, gt)
            return res

        def norm_mod_T(i, xt, SG, SH, junk, ssb, hb, hT, psT, t32, dst=None):
            k.actf(junk[:], xt[:], AF.Square, [xt], [junk, ssb], accum_out=ssb[:, 0:1])
            k.rstd(ssb[:, 2:3], ssb[:, 1:2], ssb[:, 0:1], 1.0 / D, (ssb, ssb, ssb))
            k.stt(dve, t32[:], xt[:], ssb[:, 2:3], SG[:], ALU.mult, ALU.mult, [xt, ssb, SG], [t32])
            k.tt(dve, hb[:], t32[:], SH[:], ALU.add, [t32, SH], [hb])
            for kc in range(8):
                k.tr(psT[:, kc * 128:(kc + 1) * 128], hb[:, kc * 128:(kc + 1) * 128], identb[:], [hb, identb], [psT])
            if dst is None:
                k.cp(act, hT[:].rearrange("p a b -> p (a b)"), psT[:, 0:1024], [psT], [hT])
            else:
                k.cp(act, dst, psT[:, 0:1024].rearrange("p (a b) -> p a b", a=8), [psT], [hT])

        def cast_load(st_bufs, cnt, dst_ap, dst_buf, src_ap, shape_slice, engs=None):
            stg = st_bufs[cnt[0] % len(st_bufs)]
            cnt[0] += 1
            k.dma(shape_slice(stg), src_ap, writes=[stg])
            engs = engs or (pool, dve, act)
            k.cp(engs[cnt[0] % len(engs)], dst_ap, shape_slice(stg), [stg], [dst_buf])

        def phase0():
            with contextlib.ExitStack() as st:
                cT = k.sb(st, "cT", [128, 8, 2], F32)
                sT = k.sb(st, "sT", [128, 8, 2], F32)
                wst = [k.sb(st, "adaw%d" % i, [128, 3072], F32) for i in range(2)]
                bst = k.sb(st, "adab", [1, 6 * D], F32)
                mt = k.sb(st, "mt", [2, 6 * D], F32)
                pss = [k.ps(st, "psada%d" % i, [128, 512]) for i in range(6)]
                k.dma(cT[:], c2T[:, :, :], writes=[cT])
                k.actf(sT[:], cT[:], AF.Silu, [cT], [sT])
                n = 0
                for l in range(2):
                    k.dma(bst[:], ada_b[l:l + 1, :], writes=[bst])
                    for half in range(2):
                        for kc in range(8):
                            w = wst[n % 2]
                            n += 1
                            k.dma(w[:], ada_w[l, kc * 128:(kc + 1) * 128, half * 3072:(half + 1) * 3072], writes=[w])
                            for j in range(6):
                                k.mm(pss[j][0:2, :], sT[:, kc, :], w[:, j * 512:(j + 1) * 512], kc == 0, False, [sT, w], [pss[j]])
                        for j in range(6):
                            c0 = half * 3072 + j * 512
                            k.mm(pss[j][0:2, :], ones[0:1, 0:2], bst[0:1, c0:c0 + 512], False, True, [ones, bst], [pss[j]])
                            k.cp(act, mt[0:2, c0:c0 + 512], pss[j][0:2, :], [pss[j]], [mt])
                    k.dma(mrows[l, :, :], mt[0:2, :], reads=[mt], writes=[mrows], q="pool")
            k.barrier()

        def phase_e1():
            with contextlib.ExitStack() as st:
                mod = load_mod(st, 0, 0, norm_mix)
                wb = k.sb(st, "winb", [128, 8, 2336], BF16)
                stg = [k.sb(st, "wstg%d" % i, [128, 2336], F32) for i in range(2)]
                cnt = [0]
                for kc in range(8):
                    cast_load(stg, cnt, wb[:, kc, :], wb, even_w_in[kc * 128:(kc + 1) * 128, :], lambda s: s[:, :])
                wup = k.sb(st, "wup", [33, 512], F32)
                k.dma(wup[:], gate_wup[:, :], writes=[wup])
                gqk = k.sb(st, "gqk", [128, 640], F32)
                bcast_load(gqk, qk_gain[0:1, :])
                k.ts(dve, gqk[:, 0:512], gqk[:, 0:512], 0.125, None, ALU.mult, None, [gqk], [gqk])
                cosf = k.sb(st, "cosf", [128, 32, 64], F32)
                sinf = k.sb(st, "sinf", [128, 32, 64], F32)
                k.dma(cosf[:], c_cos[:, :, :], writes=[cosf])
                k.dma(sinf[:], c_sin[:, :, :], writes=[sinf])
                R = 2
                xts = [k.sb(st, "xt%d" % i, [128, D], F32) for i in range(4)]
                junk = k.sb(st, "junk", [128, D], BF16)
                t32 = k.sb(st, "t32", [128, D], F32)
                ssb = [k.sb(st, "ssb%d" % i, [128, 4], F32) for i in range(4)]
                hb = [k.sb(st, "hb%d" % i, [128, D], BF16) for i in range(4)]
                hT = [k.sb(st, "hT%d" % i, [128, 8, 128], BF16) for i in range(3)]
                qk = [k.sb(st, "qk%d" % i, [128, 10, 64], F32) for i in range(R)]
                sq = k.sb(st, "sq", [128, 10, 64], F32)
                st10 = k.sb(st, "st10", [128, 32], F32)
                qn = k.sb(st, "qn", [128, 10, 64], F32)
                t1 = k.sb(st, "t1", [128, 10, 64], F32)
                t2 = k.sb(st, "t2", [128, 10, 64], F32)
                qrs = [k.sb(st, "qr%d" % i, [128, 10, 64], BF16) for i in range(R)]
                kds = [k.sb(st, "kd%d" % i, [128, 2, 2, 64], BF16) for i in range(R)]
                qkT = [k.sb(st, "qkT%d" % i, [128, 6, 128], BF16) for i in range(R)]
                vt = [k.sb(st, "vt%d" % i, [128, 2, 65], BF16) for i in range(R)]
                vbt = [k.sb(st, "vbt%d" % i, [128, 512], BF16) for i in range(R)]
                gbt = [k.sb(st, "gbt%d" % i, [128, 512], BF16) for i in range(R)]
                fT = [k.sb(st, "fT%d" % i, [128, 4, 128], BF16) for i in range(R)]
                lrT = [k.sb(st, "lrT%d" % i, [33, 128], F32) for i in range(R)]
                ez = k.sb(st, "ez", [128, 512], F32)
                spt = [k.sb(st, "spt%d" % i, [128, 512], F32) for i in range(R)]
                psT = k.ps(st, "psT", [128, 1024], BF16)
                psT2 = k.ps(st, "psT2e", [128, 1024], BF16)
                psQ = k.ps(st, "psQ", [128, 512])
                psK = k.ps(st, "psK", [128, 512])
                psV = k.ps(st, "psV", [128, 512])
                psG = k.ps(st, "psG", [128, 512])
                psF = k.ps(st, "psF", [128, 512])
                psZ = k.ps(st, "psZ", [128, 512])
                for v in vt:
                    k.op(dve, lambda e, v=v: e.memset(v[:], 1.0), [], [v])
                for v in lrT:
                    k.op(dve, lambda e, v=v: e.memset(v[:], 1.0), [], [v])
                def stage_a0a(i):
                    r4 = i % 4
                    is_ctx = i < 2
                    tok0 = i * 128
                    SG, SH, _ = mod[1 if is_ctx else 0]
                    xt, sb_, hbi = xts[r4], ssb[r4], hb[r4]
                    k.dma(xt[:], xc[tok0:tok0 + 128, :], writes=[xt])
                    k.actf(junk[:], xt[:], AF.Square, [xt], [junk, sb_], accum_out=sb_[:, 0:1])
                    k.rstd(sb_[:, 2:3], sb_[:, 1:2], sb_[:, 0:1], 1.0 / D, (sb_, sb_, sb_))
                    k.stt(dve, t32[:], xt[:], sb_[:, 2:3], SG[:], ALU.mult, ALU.mult, [xt, sb_, SG], [t32])
                    k.tt(dve, hbi[:], t32[:], SH[:], ALU.add, [t32, SH], [hbi])

                def stage_a0b(i):
                    hbi, hTi = hb[i % 4], hT[i % 3]
                    for kc in range(8):
                        k.tr(psT[:, kc * 128:(kc + 1) * 128], hbi[:, kc * 128:(kc + 1) * 128], identb[:], [hbi, identb], [psT])
                    k.cp(act, hTi[:].rearrange("p a b -> p (a b)"), psT[:, 0:1024], [psT], [hTi])

                def stage_a1(i):
                    r = i % R
                    tok0 = i * 128
                    h = hT[i % 3]
                    for (psb, c0, n) in ((psQ, 0, 512), (psK, 512, 256), (psV, 1280, 512), (psG, 1792, 512)):
                        for kc in range(8):
                            k.mm(psb[:, 0:n], h[:, kc, :], wb[:, kc, c0:c0 + n], kc == 0, kc == 7, [h, wb], [psb])
                    for j, c0 in enumerate((768, 896, 1024, 1152)):
                        for kc in range(8):
                            k.mm(psF[:, j * 128:(j + 1) * 128], wb[:, kc, c0:c0 + 128], h[:, kc, :], kc == 0, kc == 7, [h, wb], [psF])
                    for kc in range(8):
                        k.mm(psK[0:32, 256:384], wb[:, kc, 2304:2336], h[:, kc, :], kc == 0, kc == 7, [h, wb], [psK])
                    k.cp(act, qk[r][:, 0:8, :].rearrange("p a b -> p (a b)"), psQ[:, 0:512], [psQ], [qk[r]])
                    k.cp(act, qk[r][:, 8:10, :].rearrange("p a b -> p (a b)"), psK[:, 0:128], [psK], [qk[r]])
                    k.cp(act, vt[r][:, :, 0:64], psK[:, 128:256].rearrange("p (a b) -> p a b", a=2), [psK], [vt[r]])
                    k.dma(vA_d[tok0:tok0 + 128, :], vt[r][:].rearrange("p a b -> p (a b)"), reads=[vt[r]], writes=[vA_d], q="pool")
                    k.cp(act, vbt[r][:], psV[:, :], [psV], [vbt[r]])
                    k.dma(vB_d[tok0:tok0 + 128, :], vbt[r][:], reads=[vbt[r]], writes=[vB_d], q="pool")
                    k.cp(act, gbt[r][:], psG[:, :], [psG], [gbt[r]])
                    k.dma(gB_d[tok0:tok0 + 128, :], gbt[r][:], reads=[gbt[r]], writes=[gB_d], q="pool")
                    k.actf(fT[r][:, 0:2, :].rearrange("p a b -> p (a b)"), psF[:, 0:256], AF.Copy, [psF], [fT[r]], scale=0.125)
                    k.cp(act, fT[r][:, 2:4, :].rearrange("p a b -> p (a b)"), psF[:, 256:512], [psF], [fT[r]])
                    k.dma(qBT_d[:, :, tok0:tok0 + 128].rearrange("j p t -> p j t"), fT[r][:, 0:2, :], reads=[fT[r]], writes=[qBT_d], q="pool")
                    k.dma(kBT_d[:, :, tok0:tok0 + 128].rearrange("j p t -> p j t"), fT[r][:, 2:4, :], reads=[fT[r]], writes=[kBT_d], q="pool")
                    k.cp(act, lrT[r][0:32, :], psK[0:32, 256:384], [psK], [lrT[r]])
                def stage_b0(i):
                    r = i % R
                    is_ctx = i < 2
                    tok0 = i * 128
                    qr, kd = qrs[r], kds[r]
                    k.tt(dve, sq[:], qk[r][:], qk[r][:], ALU.mult, [qk[r]], [sq])
                    k.op(dve, lambda e: e.tensor_reduce(out=st10[:, 0:10], in_=sq[:], axis=AX.X, op=ALU.add), [sq], [st10])
                    k.rstd(st10[:, 20:30], st10[:, 10:20], st10[:, 0:10], 1.0 / 64, (st10, st10, st10))
                    k.tt(dve, qn[:], qk[r][:], st10[:, 20:30].unsqueeze(2).to_broadcast([128, 10, 64]), ALU.mult, [qk[r], st10], [qn])
                    k.tt(dve, qn[:], qn[:], gqk[:].rearrange("p (a b) -> p a b", a=10), ALU.mult, [qn, gqk], [qn])
                    if is_ctx:
                        k.cp(dve, qr[:], qn[:], [qn], [qr])
                    else:
                        li = i - 2
                        k.tt(dve, t1[:], qn[:], cosf[:, li, :].unsqueeze(1).to_broadcast([128, 10, 64]), ALU.mult, [qn, cosf], [t1])
                        qv = qn[:].rearrange("p n (a h f) -> p n a h f", a=2, h=2)
                        tv = t2[:].rearrange("p n (a h f) -> p n a h f", a=2, h=2)
                        sv = sinf[:, li, :].rearrange("p (a h f) -> p a h f", a=2, h=2)
                        for half in range(2):
                            k.tt(dve, tv[:, :, :, half, :], qv[:, :, :, 1 - half, :],
                                 sv[:, :, half, :].unsqueeze(1).to_broadcast([128, 10, 2, 16]), ALU.mult, [qn, sinf], [t2])
                        k.tt(dve, qr[:], t1[:], t2[:], ALU.add, [t1, t2], [qr])
                    k.cp(dve, kd[:], qr[:, 8:10, :].unsqueeze(2).to_broadcast([128, 2, 2, 64]), [qr], [kd])

                def stage_b1(i):
                    r = i % R
                    tok0 = i * 128
                    qr, kd = qrs[r], kds[r]
                    for j in range(4):
                        k.tr(psT2[:, j * 128:(j + 1) * 128], qr[:, 2 * j:2 * j + 2, :].rearrange("p a b -> p (a b)"), identb[:], [qr, identb], [psT2])
                    for g in range(2):
                        k.tr(psT2[:, (4 + g) * 128:(5 + g) * 128], kd[:, g, :, :].rearrange("p a b -> p (a b)"), identb[:], [kd, identb], [psT2])
                    k.cp(act, qkT[r][:].rearrange("p a b -> p (a b)"), psT2[:, 0:768], [psT2], [qkT[r]])
                    k.dma(qT_d[:, :, tok0:tok0 + 128].rearrange("j p t -> p j t"), qkT[r][:, 0:4, :], reads=[qkT[r]], writes=[qT_d], q="pool")
                    k.dma(kT_d[:, :, tok0:tok0 + 128].rearrange("j p t -> p j t"), qkT[r][:, 4:6, :], reads=[qkT[r]], writes=[kT_d], q="pool")
                    k.mm(psZ[:, :], lrT[r][0:33, :], wup[0:33, :], True, True, [lrT[r], wup], [psZ])
                    k.actf(ez[:], psZ[:, :], AF.Exp, [psZ], [ez], scale=-1.0)
                    k.actf(spt[r][:], ez[:], AF.Ln, [ez], [spt[r]], bias=1.0)
                    k.dma(sp_d[tok0:tok0 + 128, :], spt[r][:], reads=[spt[r]], writes=[sp_d], q="pool")
                for t_ in range(3):
                    stage_a0a(t_)
                stage_a0b(0)
                stage_a0b(1)
                stage_a1(0)
                for i in range(NT):
                    if i + 3 < NT:
                        stage_a0a(i + 3)
                    stage_b0(i)
                    if i + 2 < NT:
                        stage_a0b(i + 2)
                    if i + 1 < NT:
                        stage_a1(i + 1)
                    stage_b1(i)
            k.barrier()

        def phase_att():
            with contextlib.ExitStack() as st:
                qs = k.sb(st, "qs", [128, 4, TOK], BF16)
                ks = k.sb(st, "ks", [128, 2, TOK], BF16)
                vs = k.sb(st, "vs", [128, NT, 130], BF16)
                for j in range(4):
                    k.dma(qs[:, j, :], qT_d[j, :, :], reads=[qT_d], writes=[qs])
                for g in range(2):
                    k.dma(ks[:, g, :], kT_d[g, :, :], reads=[kT_d], writes=[ks])
                k.dma(vs[:], vA_d[:, :].rearrange("(c p) f -> p c f", p=128), reads=[vA_d], writes=[vs])
                NS = 2
                psS = [k.ps(st, "psS%d" % i, [128, 1024]) for i in range(NS)]
                NP = 12
                pts = [k.sb(st, "pt%d" % i, [128, 1024], BF16) for i in range(NP)]
                psO = [k.ps(st, "psO%d" % i, [128, 512]) for i in range(3)]
                psB = k.ps(st, "psB", [128, 512])
                rec = [k.sb(st, "rec%d" % i, [65, 512], F32) for i in range(3)]
                bc = [k.sb(st, "bc%d" % i, [64, 512], F32) for i in range(2)]
                ob = [k.sb(st, "ob%d" % i, [64, 512], BF16) for i in range(2)]
                blocks = [(0, 256, [0, 1])] + [(256 + 512 * j, 512, list(range(NT))) for j in range(8)]
                its = []
                pi = 0
                for (tok0, nq, kcs) in blocks:
                    for p in range(4):
                        for idx, kc in enumerate(kcs):
                            its.append((tok0, nq, p, idx, kc, len(kcs), pi))
                        pi += 1
                sidx = [0]
                sbank = {}

                def emit_S(j):
                    tok0, nq, p, idx, kc, nk, pid = its[j]
                    g = p // 2
                    pS = psS[sidx[0] % NS]
                    sbank[j] = pS
                    sidx[0] += 1
                    for hf in range(2):
                        pb = hf * 64
                        k.mm(pS[:, hf * 512:hf * 512 + nq], ks[pb:pb + 64, g, kc * 128:(kc + 1) * 128], qs[pb:pb + 64, p, tok0:tok0 + nq],
                             True, True, [ks, qs], [pS])

                def emit_exp_pv(j, hfs):
                    tok0, nq, p, idx, kc, nk, pid = its[j]
                    g = p // 2
                    pt = pts[j % NP]
                    if 0 in hfs or hfs == (0, 1):
                        pS = sbank.pop(j)
                        k.actf(pt[:].rearrange("p (a n) -> p a n", a=2)[:, :, 0:nq], pS[:].rearrange("p (a n) -> p a n", a=2)[:, :, 0:nq],
                               AF.Exp, [pS], [pt], bias=-8.0)
                    for hf in hfs:
                        pO = psO[(2 * pid + hf) % 3]
                        k.mm(pO[0:65, 0:nq], vs[:, kc, g * 65:(g + 1) * 65], pt[:, hf * 512:hf * 512 + nq], idx == 0, idx == nk - 1, [vs, pt], [pO])

                nrm = [0]

                def norm_stage(stage, info):
                    tok0, nq, h, ob_i, pO, rc, pB = info
                    b_, o_ = bc[ob_i], ob[ob_i]
                    if stage == 0:
                        k.op(dve, lambda e: e.reciprocal(out=rc[64:65, 0:nq], in_=pO[64:65, 0:nq]), [pO], [rc])
                    elif stage == 1:
                        pB[0] = psB
                        k.mm(pB[0][0:64, 0:nq], ones[64:65, 0:64], rc[64:65, 0:nq], True, True, [ones, rc], [pB[0]])
                    elif stage == 2:
                        k.cp(act, b_[0:64, 0:nq], pB[0][0:64, 0:nq], [pB[0]], [b_])
                    else:
                        k.tt(dve, o_[0:64, 0:nq], pO[0:64, 0:nq], b_[0:64, 0:nq], ALU.mult, [pO, b_], [o_])
                        k.dma(mixAT_d[h, :, tok0:tok0 + nq], o_[0:64, 0:nq], reads=[o_], writes=[mixAT_d], q="pool")

                sched = {}
                n_it = len(its)
                LOOK = 1
                DL = 8
                held = []
                long_pair = {}
                for it_ in its:
                    long_pair[it_[6]] = it_[5] >= 16
                for j in range(n_it + LOOK + 24):
                    if j < n_it:
                        emit_S(j)
                    jj = j - LOOK
                    if 0 <= jj < n_it:
                        tok0, nq, p, idx, kc, nk, pid = its[jj]
                        if long_pair[pid] and long_pair.get(pid - 1, False) and idx < DL:
                            emit_exp_pv(jj, (0,))
                            held.append(jj)
                        else:
                            for hj in held:
                                emit_exp_pv(hj, (1,))
                            held = []
                            emit_exp_pv(jj, (0, 1))
                        if idx == nk - 1:
                            for hf in range(2):
                                info = (tok0, nq, 2 * p + hf, nrm[0] % 2, psO[(2 * pid + hf) % 3], rec[(2 * pid + hf) % 3], [None])
                                nrm[0] += 1
                                for sgi in range(4):
                                    when = (j + 1 + 4 * hf + sgi) if nk >= 16 else j
                                    sched.setdefault(when, []).append((sgi, info))
                    for (sgi, info) in sched.pop(j, []):
                        norm_stage(sgi, info)
                assert not sched
            k.barrier()

        def phase_gla():
            with contextlib.ExitStack() as st:
                R = 2
                qbt = [k.sb(st, "qbt%d" % i, [128, 2, 128], BF16) for i in range(3)]
                kbt = [k.sb(st, "kbt%d" % i, [128, 2, 128], BF16) for i in range(3)]
                vb = [k.sb(st, "vb%d" % i, [128, 512], BF16) for i in range(3)]
                spd = [k.sb(st, "spd%d" % i, [128, 256], F32) for i in range(3)]
                gb = [k.sb(st, "gb%d" % i, [128, 512], BF16) for i in range(3)]
                of = [k.sb(st, "of%d" % i, [128, 512], F32) for i in range(3)]
                Eq = [k.sb(st, "Eq%d" % i, [128, 256], F32) for i in range(R)]
                Eks = [k.sb(st, "Ek%d" % i, [128, 256], F32) for i in range(R)]
                qt = [k.sb(st, "qt%d" % i, [128, 2, 128], BF16) for i in range(R)]
                kt = [k.sb(st, "kt%d" % i, [128, 2, 128], BF16) for i in range(R)]
                ktok = [k.sb(st, "ktok%d" % i, [128, 256], BF16) for i in range(R)]
                Am = [k.sb(st, "Am%d" % i, [128, 4, 128], BF16) for i in range(R)]
                S32 = [k.sb(st, "S32_%d" % i, [128, 256], F32) for i in range(2)]
                Stmp = [k.sb(st, "Stmp%d" % i, [128, 256], F32) for i in range(2)]
                Sb = [k.sb(st, "Sb%d" % i, [128, 256], BF16) for i in range(2)]
                ofs = [k.sb(st, "ofs%d" % i, [128, 512], F32) for i in range(R)]
                o32s = [k.sb(st, "o32_%d" % i, [128, 4, 128], F32) for i in range(2)]
                osq = k.sb(st, "osq", [128, 4, 128], F32)
                st4s = [k.sb(st, "st4_%d" % i, [128, 16], F32) for i in range(2)]
                sgl = k.sb(st, "sgl", [128, 512], F32)
                obfs = [k.sb(st, "obf%d" % i, [128, 4, 128], BF16) for i in range(2)]
                mbt = [k.sb(st, "mbt%d" % i, [128, 4, 128], BF16) for i in range(R)]
                og = k.sb(st, "og", [128, 512], F32)
                bcast_load(og, gla_og[0:1, :])
                psC = k.ps(st, "psC", [128, 512])
                psT = k.ps(st, "psTg", [128, 1024], BF16)
                psAx = [k.ps(st, "psA%d" % i, [128, 512]) for i in range(2)]
                psOx = [k.ps(st, "psOg%d" % i, [128, 512]) for i in range(2)]
                psS = k.ps(st, "psSg", [128, 512])
                psT2 = k.ps(st, "psT2", [128, 1024], BF16)
                n = 0
                for d in range(2):
                    order = list(range(NT)) if d == 0 else [1, 0] + list(range(NT - 1, 1, -1))
                    tri = trif if d == 0 else trib
                    msk = maskf if d == 0 else maskb
                    tl = 127 if d == 0 else 0
                    for p in range(2):
                        k.op(dve, lambda e, p=p: e.memset(S32[p][:], 0.0), [], [S32[p]])
                        k.op(dve, lambda e, p=p: e.memset(Sb[p][:], 0.0), [], [Sb[p]])
                    def front(n, i):
                        r = n % R
                        tok0 = i * 128
                        k.dma(qbt[n % 3][:], qBT_d[:, :, tok0:tok0 + 128].rearrange("j p t -> p j t"), reads=[qBT_d], writes=[qbt[n % 3]])
                        k.dma(kbt[n % 3][:], kBT_d[:, :, tok0:tok0 + 128].rearrange("j p t -> p j t"), reads=[kBT_d], writes=[kbt[n % 3]])
                        k.dma(vb[n % 3][:], vB_d[tok0:tok0 + 128, :], reads=[vB_d], writes=[vb[n % 3]])
                        k.dma(spd[n % 3][:], sp_d[tok0:tok0 + 128, d * 256:(d + 1) * 256], reads=[sp_d], writes=[spd[n % 3]])
                        if d == 1:
                            k.dma(gb[n % 3][:], gB_d[tok0:tok0 + 128, :], reads=[gB_d], writes=[gb[n % 3]])
                            k.dma(of[n % 3][:], oF_d[tok0:tok0 + 128, :], reads=[oF_d], writes=[of[n % 3]])
                        for p in range(2):
                            k.mm(psC[:, p * 128:(p + 1) * 128], spd[n % 3][:, p * 128:(p + 1) * 128], tri[:], True, True, [spd[n % 3], tri], [psC])
                        k.actf(Eq[r][:], psC[:, 0:256], AF.Exp, [psC], [Eq[r]])
                        Ek = Eks[r]
                        k.actf(Ek[:], psC[:, 0:256], AF.Exp, [psC], [Ek], scale=-1.0)
                        k.tt(dve, qt[r][:].rearrange("p a b -> p (a b)"), qbt[n % 3][:].rearrange("p a b -> p (a b)"), Eq[r][:], ALU.mult, [qbt[n % 3], Eq[r]], [qt[r]])
                        k.tt(dve, kt[r][:].rearrange("p a b -> p (a b)"), kbt[n % 3][:].rearrange("p a b -> p (a b)"), Ek[:], ALU.mult, [kbt[n % 3], Ek], [kt[r]])
                        for p in range(2):
                            k.tr(psT[:, p * 128:(p + 1) * 128], kt[r][:, p, :], identb[:], [kt[r], identb], [psT])
                        k.cp(act, ktok[r][:], psT[:, 0:256], [psT], [ktok[r]])
                        for h in range(4):
                            p, pb = h // 2, (h % 2) * 64
                            k.mm(psAx[h % 2][:, p * 128:(p + 1) * 128], kt[r][pb:pb + 64, p, :], qt[r][pb:pb + 64, p, :], True, True,
                                 [kt[r], qt[r]], [psAx[h % 2]])
                        Amv = Am[r][:].rearrange("p (a h) b -> p a h b", h=2)
                        for hf in range(2):
                            k.tt(dve, Amv[:, :, hf, :], psAx[hf][:, 0:256].rearrange("p (a b) -> p a b", a=2),
                                 msk[:].unsqueeze(1).to_broadcast([128, 2, 128]), ALU.mult, [psAx[hf], msk], [Am[r]])

                    def back(n, i):
                        r = n % R
                        tok0 = i * 128
                        for h in range(4):
                            p, hf = h // 2, h % 2
                            pb = hf * 64
                            k.mm(psOx[hf][:, p * 128:(p + 1) * 128], Am[r][:, h, :], vb[n % 3][:, h * 128:(h + 1) * 128], True, False,
                                 [Am[r], vb[n % 3]], [psOx[hf]])
                            k.mm(psOx[hf][:, p * 128:(p + 1) * 128], qt[r][pb:pb + 64, p, :], Sb[p][pb:pb + 64, hf * 128:(hf + 1) * 128], False, True,
                                 [qt[r], Sb[p]], [psOx[hf]])
                        for p in range(2):
                            k.mm(psS[:, p * 256:(p + 1) * 256], ktok[r][:, p * 128:(p + 1) * 128], vb[n % 3][:, p * 256:(p + 1) * 256], True, True,
                                 [ktok[r], vb[n % 3]], [psS])
                        for p in range(2):
                            el = Eq[r][:, p * 128 + tl:p * 128 + tl + 1]
                            k.ts(dve, Stmp[p][:], S32[p][:], el, None, ALU.mult, None, [S32[p], Eq[r]], [Stmp[p]])
                            k.stt(dve, S32[p][:], psS[:, p * 256:(p + 1) * 256], el, Stmp[p][:], ALU.mult, ALU.add, [psS, Eq[r], Stmp[p]], [S32[p]])
                            k.cp(act, Sb[p][:], S32[p][:], [S32[p]], [Sb[p]])
                        if d == 0:
                            ofv = ofs[r][:].rearrange("p (a h b) -> p a h b", a=2, h=2)
                            for hf in range(2):
                                k.cp(act, ofv[:, :, hf, :], psOx[hf][:, 0:256].rearrange("p (a b) -> p a b", a=2), [psOx[hf]], [ofs[r]])
                            k.dma(oF_d[tok0:tok0 + 128, :], ofs[r][:], reads=[ofs[r]], writes=[oF_d], q="pool")
                        else:
                            o32 = o32s[n % 2]
                            o4 = o32[:].rearrange("p (a h) b -> p a h b", h=2)
                            of4 = of[n % 3][:].rearrange("p (a h b) -> p a h b", a=2, h=2)
                            for hf in range(2):
                                k.tt(dve, o4[:, :, hf, :], psOx[hf][:, 0:256].rearrange("p (a b) -> p a b", a=2), of4[:, :, hf, :], ALU.add,
                                     [psOx[hf], of[n % 3]], [o32])

                    def out_a(n):
                        o32, obf, st4, gbn = o32s[n % 2], obfs[n % 2], st4s[n % 2], gb[n % 3]
                        o2 = o32[:].rearrange("p a b -> p (a b)")
                        k.tt(dve, osq[:], o32[:], o32[:], ALU.mult, [o32], [osq])
                        k.op(dve, lambda e: e.tensor_reduce(out=st4[:, 0:4], in_=osq[:], axis=AX.X, op=ALU.add), [osq], [st4])
                        k.rstd(st4[:, 8:12], st4[:, 4:8], st4[:, 0:4], 1.0 / 128, (st4, st4, st4))
                        k.actf(sgl[:], gbn[:], AF.Silu, [gbn], [sgl])
                        k.tt(dve, o32[:], o32[:], st4[:, 8:12].unsqueeze(2).to_broadcast([128, 4, 128]), ALU.mult, [o32, st4], [o32])
                        k.tt(dve, o2, o2, og[:], ALU.mult, [o32, og], [o32])
                        k.tt(dve, obf[:].rearrange("p a b -> p (a b)"), o2, sgl[:], ALU.mult, [o32, sgl], [obf])

                    def out_b(n, i):
                        obf, r = obfs[n % 2], n % R
                        tok0 = i * 128
                        for h in range(4):
                            k.tr(psT2[:, h * 128:(h + 1) * 128], obf[:, h, :], identb[:], [obf, identb], [psT2])
                        k.cp(act, mbt[r][:].rearrange("p a b -> p (a b)"), psT2[:, 0:512], [psT2], [mbt[r]])
                        k.dma(mixBT_d[:, :, tok0:tok0 + 128].rearrange("h p t -> p h t"), mbt[r][:], reads=[mbt[r]], writes=[mixBT_d], q="pool")

                    n0 = n
                    front(n0, order[0])
                    if len(order) > 1:
                        front(n0 + 1, order[1])
                    for jn, i in enumerate(order):
                        back(n0 + jn, i)
                        if d == 1:
                            if jn >= 1:
                                out_a(n0 + jn - 1)
                            if jn >= 2:
                                out_b(n0 + jn - 2, order[jn - 2])
                        if jn + 2 < len(order):
                            front(n0 + jn + 2, order[jn + 2])
                    if d == 1:
                        L_ = len(order)
                        out_a(n0 + L_ - 1)
                        out_b(n0 + L_ - 2, order[L_ - 2])
                        out_b(n0 + L_ - 1, order[L_ - 1])
                    n = n0 + len(order)
                    k.barrier()
        def phase_post(l, wup=None):
            with contextlib.ExitStack() as st:
                mod1 = {}
                for r_ in ((0, 1) if l == 0 else (0,)):
                    gt_ = k.sb(st, "gt1_%d" % r_, [128, D], F32)
                    k.dma(gt_[:], mrow(l, r_, 2).partition_broadcast(128), reads=[mrows], writes=[gt_])
                    mod1[r_] = (None, None, gt_)
                stg = [k.sb(st, "wstg%d" % i, [128, 2048], F32) for i in range(3)]
                cnt = [0]
                pcnt = [0]
                pieces = [(kc, c0, n_) for kc in range(8) for (c0, n_) in ((0, 2048), (2048, 2048), (4096, 1536))]
                if l == 0:
                    woA = k.sb(st, "woA", [128, 4, D], BF16)
                    woB = k.sb(st, "woB", [128, 4, D], BF16)
                    for j in range(0, 4, 2):
                        cast_load(stg, cnt, woA[:, j:j + 2, :], woA,
                                  even_w_out[j * 128:(j + 2) * 128, :].rearrange("(a p) n -> p a n", p=128),
                                  lambda s: s[:, :].rearrange("p (a n) -> p a n", a=2))
                    for j in range(0, 4, 2):
                        cast_load(stg, cnt, woB[:, j:j + 2, :], woB,
                                  even_w_out[512 + j * 128:512 + (j + 2) * 128, :].rearrange("(a p) n -> p a n", p=128),
                                  lambda s: s[:, :].rearrange("p (a n) -> p a n", a=2))
                    tiles = list(range(NT))
                else:
                    woL = k.sb(st, "woL", [128, 10, D], BF16)
                    for j in range(0, 10, 2):
                        cast_load(stg, cnt, woL[:, j:j + 2, :], woL,
                                  lru_w_out[j * 128:(j + 2) * 128, :].rearrange("(a p) n -> p a n", p=128),
                                  lambda s: s[:, :].rearrange("p (a n) -> p a n", a=2))
                    tiles = list(range(2, NT))
                mod2 = {}
                nm2 = k.sb(st, "nm2", [128, D], F32)
                bcast_load(nm2, norm_ffn[l:l + 1, :])
                for r in ((0, 1) if l == 0 else (0,)):
                    sg = k.sb(st, "sg2_%d" % r, [128, D], F32)
                    sh = k.sb(st, "sh2_%d" % r, [128, D], F32)
                    k.dma(sg[:], mrow(l, r, 4).partition_broadcast(128), reads=[mrows], writes=[sg])
                    k.dma(sh[:], mrow(l, r, 3).partition_broadcast(128), reads=[mrows], writes=[sh])
                    k.stt(dve, sg[:], sg[:], 1.0, nm2[:], ALU.add, ALU.mult, [sg, nm2], [sg])
                    mod2[r] = (sg, sh)
                R = 2
                xts = [k.sb(st, "xt%d" % i, [128, D], F32) for i in range(R)]
                x1t = [k.sb(st, "x1t%d" % i, [128, D], F32) for i in range(R)]
                junk = k.sb(st, "junk", [128, D], BF16)
                t32 = k.sb(st, "t32", [128, D], F32)
                ssb = [k.sb(st, "ssb%d" % i, [128, 4], F32) for i in range(R)]
                hb = [k.sb(st, "hb%d" % i, [128, D], BF16) for i in range(R)]
                hT = [k.sb(st, "hT%d" % i, [128, 8, 128], BF16) for i in range(R)]
                zt = k.sb(st, "zt", [128, 8, 128], BF16)
                psT = k.ps(st, "psTp", [128, 1024], BF16)
                psY = [k.ps(st, "psY%d" % i, [128, 512]) for i in range(4)]
                if l == 0:
                    ma = [k.sb(st, "ma%d" % i, [128, 4, 128], BF16) for i in range(R)]
                    mb = [k.sb(st, "mb%d" % i, [128, 4, 128], BF16) for i in range(R)]
                else:
                    mr = [k.sb(st, "mr%d" % i, [128, 10, 128], BF16) for i in range(R)]
                k.op(dve, lambda e: e.memset(zt[:], 0.0), [], [zt])
                h2v = h2T_d[:, :, :].rearrange("k p t -> p k t")
                k.dma(h2v[:, :, 0:1], zt[:, :, 0:1], reads=[zt], writes=[h2T_d], q="pool", allow_slow_non_contiguous=True)
                k.dma(h2v[:, :, 257:259], zt[:, :, 0:2], reads=[zt], writes=[h2T_d], q="pool")
                for q4 in range(4):
                    k.dma(h2v[:, :, LATC + T + 128 * q4:LATC + T + 128 * (q4 + 1)], zt[:, :, 0:128], reads=[zt], writes=[h2T_d], q="pool")
                def post_a(n, i):
                    r = n % R
                    is_ctx = i < 2
                    rr = 1 if is_ctx else 0
                    tok0 = i * 128
                    xt = xts[r]
                    src = xc if l == 0 else xc2_d
                    k.dma(xt[:], src[tok0:tok0 + 128, :], reads=[src], writes=[xt])
                    if l == 0:
                        k.dma(ma[r][:], mixAT_d[:, :, tok0:tok0 + 128].rearrange("(j q) p t -> (q p) j t", q=2), reads=[mixAT_d], writes=[ma[r]])
                        k.dma(mb[r][:], mixBT_d[:, :, tok0:tok0 + 128].rearrange("h p t -> p h t"), reads=[mixBT_d], writes=[mb[r]])
                        chunks = [(ma[r][:, j, :], woA, j, ma[r]) for j in range(4)] + [(mb[r][:, j, :], woB, j, mb[r]) for j in range(4)]
                    else:
                        lt0 = tok0 - C
                        k.dma(mr[r][:], rgT_d[:, :, lt0:lt0 + 128].rearrange("h p t -> p h t"), reads=[rgT_d], writes=[mr[r]])
                        chunks = [(mr[r][:, j, :], woL, j, mr[r]) for j in range(10)]
                    GATE = mod1[rr][2]
                    for nh in range(2):
                        pY = psY[(2 * n + nh) % 4]
                        for ci, (lh, wbuf, wi, lbuf) in enumerate(chunks):
                            kk = lh.shape[0]
                            k.mm(pY[:, :], lh, wbuf[0:kk, wi, nh * 512:(nh + 1) * 512], ci == 0, ci == len(chunks) - 1, [lbuf, wbuf], [pY])
                        sl = slice(nh * 512, (nh + 1) * 512)
                        k.tt(dve, x1t[r][:, sl], pY[:, :], GATE[:, sl], ALU.mult, [pY, GATE], [x1t[r]])
                    k.tt(dve, x1t[r][:], x1t[r][:], xt[:], ALU.add, [x1t[r], xt], [x1t[r]])
                    k.dma(x1_d[tok0:tok0 + 128, :], x1t[r][:], reads=[x1t[r]], writes=[x1_d], q="pool")

                def post_b(n, i):
                    r = n % R
                    is_ctx = i < 2
                    rr = 1 if is_ctx else 0
                    tok0 = i * 128
                    SG2, SH2 = mod2[rr]
                    norm_mod_T(i, x1t[r], SG2, SH2, junk, ssb[r], hb[r], hT[r], psT, t32)
                    col0 = (1 + tok0) if is_ctx else (LATC + tok0 - C)
                    k.dma(h2v[:, :, col0:col0 + 128], hT[r][:], reads=[hT[r]], writes=[h2T_d], q="pool")

                post_a(0, tiles[0])
                for n, i in enumerate(tiles):
                    if n + 1 < len(tiles):
                        post_a(n + 1, tiles[n + 1])
                    post_b(n, i)
                    if wup is not None and n < len(pieces):
                        kc_, c0_, n_ = pieces[n]
                        cast_load(stg, pcnt, wup[:, kc_, c0_:c0_ + n_], wup, ffn_w_up[l, kc_ * 128:(kc_ + 1) * 128, c0_:c0_ + n_],
                                  lambda s_, n_=n_: s_[:, 0:n_], engs=(pool, act))
            k.barrier()

        def phase_ffn(l, wup):
            last = l == 1
            with contextlib.ExitStack() as st:
                wdn = k.sb(st, "wdnb", [128, 22, D], BF16)
                with contextlib.ExitStack() as st2:
                    stg = [k.sb(st2, "wstg%d" % i, [128, 2048], F32) for i in range(3)]
                    cnt = [0]
                    for j in range(0, 22, 2):
                        cast_load(stg, cnt, wdn[:, j:j + 2, :], wdn,
                                  ffn_w_down[l, j * 128:(j + 2) * 128, :].rearrange("(a p) n -> p a n", p=128),
                                  lambda s: s[:, :].rearrange("p (a n) -> p a n", a=2))
                    k.barrier()
                cw = k.sb(st, "cw", [128, 44, 3], F32)
                k.dma(cw[:], ffn_conv[l, :, :, :], writes=[cw])
                gates = {}
                for r in ((0, 1) if l == 0 else (0,)):
                    gt = k.sb(st, "g2_%d" % r, [128, D], F32)
                    k.dma(gt[:], mrow(l, r, 5).partition_broadcast(128), reads=[mrows], writes=[gt])
                    gates[r] = gt
                if last:
                    fg = k.sb(st, "fg", [128, D], F32)
                    bcast_load(fg, final_gain[0:1, :])
                hTb = [k.sb(st, "hTb%d" % i, [128, 8, 384], BF16) for i in range(2)]
                NU = 4
                psU = [k.ps(st, "psU%d" % i, [128, 512]) for i in range(NU)]
                tcv = [k.sb(st, "tcv%d" % i, [128, 384], F32) for i in range(4)]
                sgt = [k.sb(st, "sgt%d" % i, [128, 384], F32) for i in range(2)]
                G = k.sb(st, "G", [128, 22, 384], BF16)
                psY = [k.ps(st, "psYf%d" % i, [128, 512]) for i in range(4)]
                x1t = [k.sb(st, "x1f%d" % i, [128, D], F32) for i in range(1)] * 2
                x2t = [k.sb(st, "x2f%d" % i, [128, D], F32) for i in range(1)] * 2
                junk = k.sb(st, "junkf", [128, D], BF16)
                ssb = [k.sb(st, "ssf%d" % i, [128, 4], F32) for i in range(2)]
                h2v = h2T_d[:, :, :].rearrange("k p t -> p k t")
                blocks = []
                if l == 0:
                    blocks.append((0, 256, 0, 1))
                s = 0
                while s < T:
                    nv = min(382, T - s)
                    blocks.append((LATC - 1 + s, nv, C + s, 0))
                    s += nv
                nu = 0
                ng = 0
                for bi, (col0, nv, tokb, rr) in enumerate(blocks):
                    N = nv + 2
                    hb_ = hTb[bi % 2]
                    k.dma(hb_[:, :, 0:N], h2v[:, :, col0:col0 + N], reads=[h2T_d], writes=[hb_])
                    for c in range(22):
                        tv = []
                        for which in range(2):
                            cc = c + 22 * which
                            pU = psU[nu % NU]
                            t = tcv[nu % 4]
                            nu += 1
                            for kc in range(8):
                                k.mm(pU[:, 0:N], wup[:, kc, cc * 128:(cc + 1) * 128], hb_[:, kc, 0:N], kc == 0, kc == 7, [wup, hb_], [pU])
                            k.actf(t[:, 0:nv], pU[:, 0:nv], AF.Copy, [pU, cw], [t], scale=cw[:, cc, 0:1])
                            k.stt(dve, t[:, 0:nv], pU[:, 1:nv + 1], cw[:, cc, 1:2], t[:, 0:nv], ALU.mult, ALU.add, [pU, cw, t], [t])
                            k.stt(dve, t[:, 0:nv], pU[:, 2:nv + 2], cw[:, cc, 2:3], t[:, 0:nv], ALU.mult, ALU.add, [pU, cw, t], [t])
                            tv.append(t)
                        sg = sgt[c % 2]
                        k.actf(sg[:, 0:nv], tv[0][:, 0:nv], AF.Silu, [tv[0]], [sg])
                        k.tt(pool, G[:, c, 0:nv], sg[:, 0:nv], tv[1][:, 0:nv], ALU.mult, [sg, tv[1]], [G])
                    ngrp = (nv + 127) // 128
                    for m in range(ngrp):
                        nt = min(128, nv - 128 * m)
                        tk = tokb + 128 * m
                        r = ng % 2
                        ng += 1
                        k.dma(x1t[r][0:nt, :], x1_d[tk:tk + nt, :], reads=[x1_d], writes=[x1t[r]])
                        for nh in range(2):
                            pY = psY[(2 * ng + nh) % 4]
                            for c in range(22):
                                k.mm(pY[0:nt, :], G[:, c, 128 * m:128 * m + nt], wdn[:, c, nh * 512:(nh + 1) * 512], c == 0, c == 21, [G, wdn], [pY])
                            sl = slice(nh * 512, (nh + 1) * 512)
                            k.tt(dve, x2t[r][0:nt, sl], pY[0:nt, :], gates[rr][0:nt, sl], ALU.mult, [pY, gates[rr]], [x2t[r]])
                        k.tt(dve, x2t[r][0:nt, :], x2t[r][0:nt, :], x1t[r][0:nt, :], ALU.add, [x2t[r], x1t[r]], [x2t[r]])
                        if not last:
                            k.dma(xc2_d[tk:tk + nt, :], x2t[r][0:nt, :], reads=[x2t[r]], writes=[xc2_d], q="pool")
                        else:
                            sb_ = ssb[r]
                            k.actf(junk[0:nt, :], x2t[r][0:nt, :], AF.Square, [x2t[r]], [junk, sb_], accum_out=sb_[0:nt, 0:1])
                            k.rstd(sb_[0:nt, 2:3], sb_[0:nt, 1:2], sb_[0:nt, 0:1], 1.0 / D, (sb_, sb_, sb_))
                            k.stt(dve, x1t[r][0:nt, :], x2t[r][0:nt, :], sb_[0:nt, 2:3], fg[0:nt, :], ALU.mult, ALU.mult, [x2t[r], sb_, fg], [x1t[r]])
                            k.dma(out_d[tk - C:tk - C + nt, :], x1t[r][0:nt, :], reads=[x1t[r]], writes=[out_d], q="pool")
            k.barrier()
        def phase_lru():
            Y_d = scratch("Y_d", [10, 128, TOK], F32)
            GL_d = scratch("GL_d", [10, 128, T], BF16)
            with contextlib.ExitStack() as st:
                mod = load_mod(st, 1, 0, norm_mix)
                wl = k.sb(st, "wlin", [128, 8, 2560], BF16)
                with contextlib.ExitStack() as st2:
                    stg = [k.sb(st2, "wstg%d" % i, [128, 2560], F32) for i in range(2)]
                    cnt = [0]
                    for kc in range(8):
                        cast_load(stg, cnt, wl[:, kc, :], wl, lru_w_in[kc * 128:(kc + 1) * 128, :], lambda s_: s_[:, :])
                    k.barrier()
                junk = k.sb(st, "junk", [128, D], BF16)
                t32 = k.sb(st, "t32", [128, D], F32)
                hblk = [k.sb(st, "hblk%d" % i, [128, 8, 512], BF16) for i in range(2)]
                yb = [k.sb(st, "yb%d" % i, [128, 512], F32) for i in range(3)]
                gbl = [k.sb(st, "gbl%d" % i, [128, 512], BF16) for i in range(3)]
                psT = k.ps(st, "psTl", [128, 1024], BF16)
                psR = [k.ps(st, "psRp%d" % i, [128, 512]) for i in range(3)]
                psG = [k.ps(st, "psGp%d" % i, [128, 512]) for i in range(3)]
                blocks = [(0, 256)] + [(C + 512 * j, 512) for j in range(8)]
                ne = 0

                xts4 = [k.sb(st, "xq%d" % i, [128, D], F32) for i in range(4)]
                hb4 = [k.sb(st, "hq%d" % i, [128, D], BF16) for i in range(4)]
                ss4 = [k.sb(st, "sq%d" % i, [128, 4], F32) for i in range(4)]

                def fill_a(bi):
                    t0, n = blocks[bi]
                    for j in range(n // 128):
                        tok0 = t0 + j * 128
                        SG, SH, _ = mod[1 if tok0 < C else 0]
                        xt, sb_, hbj = xts4[j], ss4[j], hb4[j]
                        k.dma(xt[:], xc2_d[tok0:tok0 + 128, :], reads=[xc2_d], writes=[xt])
                        k.actf(junk[:], xt[:], AF.Square, [xt], [junk, sb_], accum_out=sb_[:, 0:1])
                        k.rstd(sb_[:, 2:3], sb_[:, 1:2], sb_[:, 0:1], 1.0 / D, (sb_, sb_, sb_))
                        k.stt(dve, t32[:], xt[:], sb_[:, 2:3], SG[:], ALU.mult, ALU.mult, [xt, sb_, SG], [t32])
                        k.tt(dve, hbj[:], t32[:], SH[:], ALU.add, [t32, SH], [hbj])

                def fill_b(bi, j):
                    t0, n = blocks[bi]
                    if j >= n // 128:
                        return
                    hk, hbj = hblk[bi % 2], hb4[j]
                    for kc in range(8):
                        k.tr(psT[:, kc * 128:(kc + 1) * 128], hbj[:, kc * 128:(kc + 1) * 128], identb[:], [hbj, identb], [psT])
                    k.cp(act, hk[:, :, j * 128:(j + 1) * 128], psT[:, 0:1024].rearrange("p (a b) -> p a b", a=8), [psT], [hk])

                fill_a(0)
                for j in range(4):
                    fill_b(0, j)
                for bi, (t0, n) in enumerate(blocks):
                    if bi + 1 < len(blocks):
                        fill_a(bi + 1)
                    hk = hblk[bi % 2]
                    for ch in range(10):
                        e = ne % 3
                        ne += 1
                        for kc in range(8):
                            k.mm(psR[e][:, 0:n], wl[:, kc, 1280 + ch * 128:1280 + (ch + 1) * 128], hk[:, kc, 0:n], kc == 0, kc == 7, [wl, hk], [psR[e]])
                        k.cp(act, yb[e][:, 0:n], psR[e][:, 0:n], [psR[e]], [yb[e]])
                        k.dma(Y_d[ch, :, t0:t0 + n], yb[e][:, 0:n], reads=[yb[e]], writes=[Y_d], q="pool")
                        if t0 >= C:
                            for kc in range(8):
                                k.mm(psG[e][:, 0:n], wl[:, kc, ch * 128:(ch + 1) * 128], hk[:, kc, 0:n], kc == 0, kc == 7, [wl, hk], [psG[e]])
                            k.actf(gbl[e][:, 0:n], psG[e][:, 0:n], AF.Gelu_apprx_tanh, [psG[e]], [gbl[e]])
                            k.dma(GL_d[ch, :, t0 - C:t0 - C + n], gbl[e][:, 0:n], reads=[gbl[e]], writes=[GL_d], q="pool")
                        if bi + 1 < len(blocks) and ch % 2 == 1 and ch // 2 < 4:
                            fill_b(bi + 1, ch // 2)
            k.barrier()
            with contextlib.ExitStack() as st:
                YW = TOK + 6
                Y = [k.sb(st, "Y%d" % i, [128, YW], F32) for i in range(2)]
                U = k.sb(st, "U", [128, TOK], F32)
                Ubs = [k.sb(st, "Ub%d" % i, [128, TOK], BF16) for i in range(2)]
                A = k.sb(st, "A", [128, TOK], F32)
                A1 = k.sb(st, "A1", [128, TOK], F32)
                X = k.sb(st, "X", [128, TOK], F32)
                Hf = k.sb(st, "Hf", [128, TOK], F32)
                RT = k.sb(st, "RT", [128, TOK], F32)
                IT = k.sb(st, "IT", [128, TOK], F32)
                GL = [k.sb(st, "GL%d" % i, [128, T], BF16) for i in range(2)]
                wg4 = [[k.sb(st, "wg4_%d_%d" % (j, i), [128, 128], BF16) for i in range(4)] for j in range(2)]
                stg = [k.sb(st, "lstg%d" % i, [128, 128], F32) for i in range(2)]
                cwl = k.sb(st, "cwl", [128, 10, 4], F32)
                lam = k.sb(st, "lam", [128, 2, 10], F32)
                sc8 = k.sb(st, "sc8", [128, 2, 10], F32)
                sc16 = k.sb(st, "sc16", [128, 2, 10], F32)
                ba = k.sb(st, "ba", [128, 2, 10], F32)
                bx = k.sb(st, "bx", [128, 2, 10], F32)
                psA = [k.ps(st, "psAl%d" % i, [128, 512]) for i in range(2)]
                psX = [k.ps(st, "psXl%d" % i, [128, 512]) for i in range(2)]
                k.dma(cwl[:], lru_conv[:, :, :], writes=[cwl])
                k.dma(lam[:], lru_lam[:, :, :], writes=[lam])
                k.dma(ba[:], lru_b_a[:, :, :], writes=[ba])
                k.dma(bx[:], lru_b_x[:, :, :], writes=[bx])
                k.actf(sc8[:], lam[:], AF.Exp, [lam], [sc8], scale=-1.0)
                k.actf(sc16[:], sc8[:], AF.Ln, [sc8], [sc16], bias=1.0)
                k.ts(dve, sc8[:], sc16[:], -8.0, None, ALU.mult, None, [sc16], [sc8])
                k.ts(dve, sc16[:], sc16[:], -16.0, None, ALU.mult, None, [sc16], [sc16])
                blocks = [(0, 256)] + [(C + 512 * j, 512) for j in range(8)]
                cnt = [0]
                nbc = [0]

                def pass1(ch):
                    w = ch % 2
                    Yc, GLc = Y[w], GL[w]
                    for d in range(2):
                        cast_load(stg, cnt, wg4[w][d][:], wg4[w][d], lru_w_a[d, ch, :, :], lambda s_: s_[:, :], engs=(pool, act))
                        cast_load(stg, cnt, wg4[w][2 + d][:], wg4[w][2 + d], lru_w_x[d, ch, :, :], lambda s_: s_[:, :], engs=(pool, act))
                    for (a_, b_) in ((0, 1), (257, 260), (YW - 2, YW)):
                        k.op(pool, lambda e, a_=a_, b_=b_: e.memset(Yc[:, a_:b_], 0.0), [], [Yc])
                    k.dma(Yc[:, 1:1 + C], Y_d[ch, :, 0:C], reads=[Y_d], writes=[Yc])
                    k.dma(Yc[:, 260:260 + T], Y_d[ch, :, C:TOK], reads=[Y_d], writes=[Yc])
                    k.dma(GLc[:], GL_d[ch, :, :], reads=[GL_d], writes=[GLc])

                def conv(ch):
                    Yc, Ubc = Y[ch % 2], Ubs[ch % 2]
                    for (c0, L, uo) in ((1, C, 0), (260, T, C)):
                        k.actf(U[:, uo:uo + L], Yc[:, c0 - 1:c0 - 1 + L], AF.Copy, [Yc, cwl], [U], scale=cwl[:, ch, 0:1])
                        for tap in (1, 2):
                            k.stt(dve, U[:, uo:uo + L], Yc[:, c0 - 1 + tap:c0 - 1 + tap + L], cwl[:, ch, tap:tap + 1], U[:, uo:uo + L],
                                  ALU.mult, ALU.add, [Yc, cwl, U], [U])
                        k.stt(dve, Ubc[:, uo:uo + L], Yc[:, c0 + 2:c0 + 2 + L], cwl[:, ch, 3:4], U[:, uo:uo + L],
                              ALU.mult, ALU.add, [Yc, cwl, U], [Ubc])

                pass1(0)
                conv(0)
                for ch in range(10):
                    w = ch % 2
                    Yc, GLc, Ub = Y[w], GL[w], Ubs[w]
                    if ch + 1 < 10:
                        pass1(ch + 1)
                    for d in range(2):
                        Ad = A if d == 0 else A1
                        Id = IT if d == 0 else X
                        for (t0, n) in blocks:
                            r = nbc[0] % 2
                            nbc[0] += 1
                            k.mm(psA[r][:, 0:n], wg4[w][d][:], Ub[:, t0:t0 + n], True, True, [wg4[w][d], Ub], [psA[r]])
                            k.mm(psX[r][:, 0:n], wg4[w][2 + d][:], Ub[:, t0:t0 + n], True, True, [wg4[w][2 + d], Ub], [psX[r]])
                            k.actf(RT[:, t0:t0 + n], psA[r][:, 0:n], AF.Sigmoid, [psA[r], ba], [RT], bias=ba[:, d, ch:ch + 1])
                            k.actf(Id[:, t0:t0 + n], psX[r][:, 0:n], AF.Sigmoid, [psX[r], bx], [Id], bias=bx[:, d, ch:ch + 1])
                        k.tt(pool, Id[:], Id[:], Ub[:], ALU.mult, [Id, Ub], [Id])
                        k.actf(Ad[:], RT[:], AF.Exp, [RT, sc8], [Ad], scale=sc8[:, d, ch:ch + 1])
                        k.actf(RT[:], RT[:], AF.Exp, [RT, sc16], [RT], scale=sc16[:, d, ch:ch + 1])
                        k.actf(RT[:], RT[:], AF.Sqrt, [RT], [RT], scale=-1.0, bias=1.0)
                        k.tt(dve, Id[:], RT[:], Id[:], ALU.mult, [RT, Id], [Id])
                        if d == 0:
                            k.op(dve, lambda e: e.tensor_tensor_scan(out=Hf[:, 0:C], data0=Ad[:, 0:C], data1=Id[:, 0:C], initial=0.0,
                                                                    op0=ALU.mult, op1=ALU.add), [Ad, Id], [Hf])
                            k.op(dve, lambda e: e.tensor_tensor_scan(out=Hf[:, C:TOK], data0=Ad[:, C:TOK], data1=Id[:, C:TOK],
                                                                    initial=Hf[:, C - 1:C], op0=ALU.mult, op1=ALU.add), [Ad, Id, Hf], [Hf])
                            if ch + 1 < 10:
                                conv(ch + 1)
                        else:
                            k.op(dve, lambda e: e.tensor_tensor_scan(out=Yc[:, 0:C][:, ::-1], data0=Ad[:, 0:C][:, ::-1], data1=Id[:, 0:C][:, ::-1],
                                                                    initial=0.0, op0=ALU.mult, op1=ALU.add), [Ad, Id], [Yc])
                            k.op(dve, lambda e: e.tensor_tensor_scan(out=Yc[:, C:TOK][:, ::-1], data0=Ad[:, C:TOK][:, ::-1],
                                                                    data1=Id[:, C:TOK][:, ::-1], initial=Yc[:, 0:1],
                                                                    op0=ALU.mult, op1=ALU.add), [Ad, Id, Yc], [Yc])
                    k.tt(dve, Hf[:, C:TOK], Hf[:, C:TOK], Yc[:, C:TOK], ALU.add, [Hf, Yc], [Hf])
                    k.tt(pool, GLc[:], Hf[:, C:TOK], GLc[:], ALU.mult, [Hf, GLc], [GLc])
                    k.dma(rgT_d[ch, :, :], GLc[:], reads=[GLc], writes=[rgT_d], q="pool")
            k.barrier()

        def phase_post_ffn(l):
            with contextlib.ExitStack() as ost:
                wup = k.sb(ost, "wupb", [128, 8, 2 * DFF], BF16)
                phase_post(l, wup)
                if STOP_AFTER == "post%d" % l:
                    return
                phase_ffn(l, wup)

        stages = [("p0", phase0), ("e1", phase_e1), ("att", phase_att), ("gla", phase_gla),
                  ("post0,ffn0", lambda: phase_post_ffn(0)), ("lru", phase_lru), ("post1,ffn1", lambda: phase_post_ffn(1))]
        only = os.environ.get("MK_ONLY", "")
        for name, fn in stages:
            if only and not (set(name.split(",")) & set(only.split(","))):
                continue
            fn()
            if STOP_AFTER and STOP_AFTER in name.split(","):
                break
        k.barrier()
    return nc, dbg


def _host_inputs(inp, b):
    f = np.float32
    m = {}
    m["xc"] = np.ascontiguousarray(np.concatenate([inp["ctx"][b], inp["x"][b]], axis=0), dtype=f)
    c2 = np.stack([inp["c"][b], inp["c_ctx"]], axis=1)
    m["c2T"] = np.ascontiguousarray(c2.reshape(8, 128, 2).transpose(1, 0, 2), dtype=f)
    for nme in ("ada_w", "ada_b", "norm_mix", "norm_ffn", "ffn_w_up", "ffn_w_down"):
        m[nme] = np.ascontiguousarray(inp[nme], dtype=f)
    m["ffn_conv"] = np.ascontiguousarray(inp["ffn_conv"].reshape(2, 3, 44, 128).transpose(0, 3, 2, 1), dtype=f)
    m["even_w_in"] = np.ascontiguousarray(inp["even_w_in"][0], dtype=f)
    m["even_w_out"] = np.ascontiguousarray(inp["even_w_out"][0], dtype=f)
    m["qk_gain"] = np.ascontiguousarray(np.concatenate([np.tile(inp["attn_q_gain"][0], 8), np.tile(inp["attn_k_gain"][0], 2)])[None, :], dtype=f)
    gw = np.zeros((33, 512), f)
    gw[0:16, 0:256] = inp["gla_gate_w_up"][0, 0]
    gw[16:32, 256:512] = inp["gla_gate_w_up"][0, 1]
    gw[32, 0:256] = inp["gla_gate_b"][0, 0]
    gw[32, 256:512] = inp["gla_gate_b"][0, 1]
    m["gate_wup"] = gw
    m["gla_og"] = np.ascontiguousarray(np.tile(inp["gla_out_gain"][0], 4)[None, :], dtype=f)
    m["lru_w_in"] = np.ascontiguousarray(inp["lru_w_in"][0], dtype=f)
    m["lru_conv"] = np.ascontiguousarray(inp["lru_conv"][0].reshape(4, 10, 128).transpose(2, 1, 0), dtype=f)
    for nme, src in (("lru_lam", "lru_lambda"), ("lru_b_a", "lru_b_a"), ("lru_b_x", "lru_b_x")):
        m[nme] = np.ascontiguousarray(inp[src][0].reshape(2, 10, 128).transpose(2, 0, 1), dtype=f)
    m["lru_w_a"] = np.ascontiguousarray(inp["lru_w_a"][0], dtype=f)
    m["lru_w_x"] = np.ascontiguousarray(inp["lru_w_x"][0], dtype=f)
    m["lru_w_out"] = np.ascontiguousarray(inp["lru_w_out"][0], dtype=f)
    m["final_gain"] = np.ascontiguousarray(inp["final_gain"][None, :], dtype=f)
    return m


def _consts():
    f = np.float32
    s = np.arange(128)[:, None]
    t = np.arange(128)[None, :]
    c = {"c_ident": np.eye(128, dtype=f), "c_maskf": (s <= t).astype(f), "c_maskb": (s >= t).astype(f)}
    tok = (np.arange(32)[None, :] * 128 + np.arange(128)[:, None]).astype(np.int64)
    row = (tok // 64).astype(f)
    col = (tok % 64).astype(f)
    inv = (np.float32(10000.0) ** (-np.arange(16, dtype=f) / np.float32(16))).astype(f)
    ang = np.concatenate([row[:, :, None] * inv[None, None, :], col[:, :, None] * inv[None, None, :]], axis=-1).astype(f)
    cs, sn = np.cos(ang), np.sin(ang)
    cosf = np.concatenate([cs[..., 0:16], cs[..., 0:16], cs[..., 16:32], cs[..., 16:32]], axis=-1)
    sinf = np.concatenate([-sn[..., 0:16], sn[..., 0:16], -sn[..., 16:32], sn[..., 16:32]], axis=-1)
    c["c_cos"] = np.ascontiguousarray(cosf, dtype=f)
    c["c_sin"] = np.ascontiguousarray(sinf, dtype=f)
    return c


_CACHE = {}


def kernel(**inputs):
    inp = {k_: np.asarray(v) for k_, v in inputs.items()}
    if "nc" not in _CACHE:
        _CACHE["nc"] = build_program()
    nc, dbg = _CACHE["nc"]
    consts = _consts()
    ncores = int(os.environ.get("MK_CORES", "8"))
    in_maps = []
    for b in range(ncores):
        m = _host_inputs(inp, b)
        m.update(consts)
        in_maps.append(m)
    res = run_bass_kernel_spmd(nc, in_maps, core_ids=list(range(ncores)))
    if DEBUG:
        _CACHE["res"] = res
    out = np.stack([np.asarray(r["out"], dtype=np.float32) for r in res.results], axis=0)
    return out
```

```python
import os
import contextlib
import numpy as np
import ml_dtypes
import concourse.bass as bass
import concourse.mybir as mybir
from concourse.bass_utils import run_bass_kernel_spmd

F32 = mybir.dt.float32
BF16 = mybir.dt.bfloat16
AF = mybir.ActivationFunctionType
ALU = mybir.AluOpType
AX = mybir.AxisListType

D = 1024
T = 4096
C = 256
TOK = T + C
NT = TOK // 128
EPS = 1e-6
DFF = 2816
HW = 4868
LATC = 259
DEBUG = os.environ.get("MK_DEBUG", "") != ""
STOP_AFTER = os.environ.get("MK_STOP", "")


class Buf:
    __slots__ = ("t", "w", "r", "name")

    def __init__(self, t, name=""):
        self.t = t
        self.w = None
        self.r = {}
        self.name = name

    def __getitem__(self, key):
        return self.t[key]


class Eng:
    def __init__(self, k, name, raw, skip_own=False):
        self.k = k
        self.name = name
        self.raw = raw
        self.skip_own = skip_own
        self.sem = k.new_sem("e_" + name)
        self.cnt = 0
        self.seen = {}

    def wait_tok(self, tok):
        sem, val = tok
        if self.seen.get(id(sem), 0) >= val:
            return
        self.raw.wait_ge(sem, val)
        self.seen[id(sem)] = val


class K:
    def __init__(self, nc, ndma=12):
        self.nc = nc
        self.es = contextlib.ExitStack()
        self.pe = Eng(self, "pe", nc.tensor, skip_own=True)
        self.act = Eng(self, "act", nc.scalar)
        self.dve = Eng(self, "dve", nc.vector)
        self.pool = Eng(self, "pool", nc.gpsimd)
        self.sp = Eng(self, "sp", nc.sync)
        self.engs = [self.pe, self.act, self.dve, self.pool, self.sp]
        self.dq = {}
        for qn, e in (("sp", self.sp), ("pool", self.pool)):
            self.dq[qn] = dict(eng=e, sems=[[self.new_sem("d_%s%d" % (qn, i)), 0] for i in range(ndma)], idx=0)
        self.n_inst = 0

    def new_sem(self, name):
        return self.es.enter_context(self.nc.semaphore(name))

    def sb(self, stack, name, shape, dtype):
        self.uid = getattr(self, "uid", 0) + 1
        name = "%s_%d" % (name, self.uid)
        return Buf(stack.enter_context(self.nc.sbuf_tensor(name, list(shape), dtype)), name)

    def ps(self, stack, name, shape, dtype=F32):
        self.uid = getattr(self, "uid", 0) + 1
        name = "%s_%d" % (name, self.uid)
        return Buf(stack.enter_context(self.nc.psum_tensor(name, list(shape), dtype)), name)

    def dram(self, name, shape, dtype, kind="Internal"):
        return Buf(self.nc.dram_tensor(name, list(shape), dtype, kind=kind).ap(), name)

    def _deps(self, eng, reads, writes):
        for b in reads:
            if b.w is not None and not (eng.skip_own and b.w[0] is eng.sem):
                eng.wait_tok(b.w)
        for b in writes:
            if b.w is not None and not (eng.skip_own and b.w[0] is eng.sem):
                eng.wait_tok(b.w)
            for tok in b.r.values():
                if not (eng.skip_own and tok[0] is eng.sem):
                    eng.wait_tok(tok)

    def _mark(self, tok, key, reads, writes):
        for b in reads:
            b.r[key] = tok
        for b in writes:
            b.w = tok
            b.r = {}

    def op(self, eng, fn, reads=(), writes=()):
        self._deps(eng, reads, writes)
        inst = fn(eng.raw)
        eng.cnt += 1
        inst.then_inc(eng.sem, 1)
        self._mark((eng.sem, eng.cnt), eng.name, reads, writes)
        self.n_inst += 1

    def dma(self, out, in_, reads=(), writes=(), q="sp", **kw):
        Q = self.dq[q]
        eng = Q["eng"]
        self._deps(eng, reads, writes)
        slot = Q["sems"][Q["idx"] % len(Q["sems"])]
        Q["idx"] += 1
        if slot[1] > 0:
            eng.wait_tok((slot[0], slot[1]))
        slot[1] += 16
        eng.raw.dma_start(out=out, in_=in_, **kw).then_inc(slot[0], 16)
        tok = (slot[0], slot[1])
        self._mark(tok, "dma_%s_%d" % (q, id(slot)), reads, writes)
        self.n_inst += 1

    def barrier(self):
        toks = [(e.sem, e.cnt) for e in self.engs if e.cnt > 0]
        for Q in self.dq.values():
            for s in Q["sems"]:
                if s[1] > 0:
                    toks.append((s[0], s[1]))
        for e in self.engs:
            for t in toks:
                if t[0] is not e.sem:
                    e.wait_tok(t)

    def mm(self, out, lhsT, rhs, start, stop, reads, writes):
        self.op(self.pe, lambda e: e.matmul(out, lhsT=lhsT, rhs=rhs, start=start, stop=stop), reads, writes)

    def tr(self, out, in_, ident, reads, writes):
        self.op(self.pe, lambda e: e.transpose(out, in_, ident), reads, writes)

    def actf(self, out, in_, func, reads, writes, **kw):
        self.op(self.act, lambda e: e.activation(out=out, in_=in_, func=func, **kw), reads, writes)

    def tt(self, eng, out, in0, in1, op, reads, writes):
        self.op(eng, lambda e: e.tensor_tensor(out=out, in0=in0, in1=in1, op=op), reads, writes)

    def ts(self, eng, out, in0, s1, s2, op0, op1, reads, writes):
        if s2 is None:
            self.op(eng, lambda e: e.tensor_scalar(out=out, in0=in0, scalar1=s1, scalar2=None, op0=op0), reads, writes)
        else:
            self.op(eng, lambda e: e.tensor_scalar(out=out, in0=in0, scalar1=s1, scalar2=s2, op0=op0, op1=op1), reads, writes)

    def stt(self, eng, out, in0, scalar, in1, op0, op1, reads, writes):
        self.op(eng, lambda e: e.scalar_tensor_tensor(out=out, in0=in0, scalar=scalar, in1=in1, op0=op0, op1=op1), reads, writes)

    def cp(self, eng, out, in_, reads, writes):
        if eng is self.act:
            self.op(eng, lambda e: e.copy(out=out, in_=in_), reads, writes)
        else:
            self.op(eng, lambda e: e.tensor_copy(out=out, in_=in_), reads, writes)

    def rstd(self, out, tmp, ss, inv_n, reads_ss):
        b_ss, b_tmp, b_out = reads_ss
        self.actf(tmp, ss, AF.Ln, [b_ss], [b_tmp], scale=inv_n, bias=EPS)
        self.actf(out, tmp, AF.Exp, [b_tmp], [b_out], scale=-0.5)


def build_program():
    nc = bass.Bass("TRN2", target_bir_lowering=False)
    k = K(nc)

    def din(name, shape, dt=F32):
        return Buf(nc.dram_tensor(name, list(shape), dt, kind="ExternalInput").ap(), name)

    xc = din("xc", [TOK, D])
    c2T = din("c2T", [128, 8, 2])
    ada_w = din("ada_w", [2, D, 6 * D])
    ada_b = din("ada_b", [2, 6 * D])
    norm_mix = din("norm_mix", [2, D])
    norm_ffn = din("norm_ffn", [2, D])
    ffn_w_up = din("ffn_w_up", [2, D, 2 * DFF])
    ffn_conv = din("ffn_conv", [2, 128, 44, 3])
    ffn_w_down = din("ffn_w_down", [2, DFF, D])
    even_w_in = din("even_w_in", [D, 2336])
    even_w_out = din("even_w_out", [D, D])
    qk_gain = din("qk_gain", [1, 640])
    gate_wup = din("gate_wup", [33, 512])
    gla_og = din("gla_og", [1, 512])
    lru_w_in = din("lru_w_in", [D, 2560])
    lru_conv = din("lru_conv", [128, 10, 4])
    lru_lam = din("lru_lam", [128, 2, 10])
    lru_w_a = din("lru_w_a", [2, 10, 128, 128])
    lru_b_a = din("lru_b_a", [128, 2, 10])
    lru_w_x = din("lru_w_x", [2, 10, 128, 128])
    lru_b_x = din("lru_b_x", [128, 2, 10])
    lru_w_out = din("lru_w_out", [1280, D])
    final_gain = din("final_gain", [1, D])
    c_ident = din("c_ident", [128, 128])
    c_maskf = din("c_maskf", [128, 128])
    c_maskb = din("c_maskb", [128, 128])
    c_cos = din("c_cos", [128, 32, 64])
    c_sin = din("c_sin", [128, 32, 64])
    out_d = Buf(nc.dram_tensor("out", [T, D], F32, kind="ExternalOutput").ap(), "out")

    dbg = {}

    def scratch(name, shape, dt):
        kind = "ExternalOutput" if DEBUG else "Internal"
        b = Buf(nc.dram_tensor(name, list(shape), dt, kind=kind).ap(), name)
        dbg[name] = b
        return b

    mrows = scratch("mrows", [2, 2, 6 * D], F32)
    qT_d = scratch("qT_d", [4, 128, TOK], BF16)
    kT_d = scratch("kT_d", [2, 128, TOK], BF16)
    vA_d = scratch("vA_d", [TOK, 130], BF16)
    qBT_d = scratch("qBT_d", [2, 128, TOK], BF16)
    kBT_d = scratch("kBT_d", [2, 128, TOK], BF16)
    vB_d = scratch("vB_d", [TOK, 512], BF16)
    gB_d = scratch("gB_d", [TOK, 512], BF16)
    sp_d = scratch("sp_d", [TOK, 512], F32)
    oF_d = scratch("oF_d", [TOK, 512], F32)
    mixAT_d = scratch("mixAT_d", [8, 64, TOK], BF16)
    mixBT_d = scratch("mixBT_d", [4, 128, TOK], BF16)
    x1_d = scratch("x1_d", [TOK, D], F32)
    h2T_d = scratch("h2T_d", [8, 128, HW], BF16)
    xc2_d = scratch("xc2_d", [TOK, D], F32)
    rgT_d = scratch("rgT_d", [10, 128, T], BF16)

    pe, act, dve, pool = k.pe, k.act, k.dve, k.pool

    with k.es, contextlib.ExitStack() as cst:
        identf = k.sb(cst, "identf", [128, 128], F32)
        identb = k.sb(cst, "identb", [128, 128], BF16)
        maskf = k.sb(cst, "maskf", [128, 128], F32)
        maskb = k.sb(cst, "maskb", [128, 128], F32)
        trif = k.sb(cst, "trif", [128, 128], F32)
        trib = k.sb(cst, "trib", [128, 128], F32)
        ones = k.sb(cst, "ones", [128, 128], F32)
        k.dma(identf[:], c_ident[:, :], writes=[identf])
        k.dma(maskf[:], c_maskf[:, :], writes=[maskf])
        k.dma(maskb[:], c_maskb[:, :], writes=[maskb])
        k.cp(dve, identb[:], identf[:], [identf], [identb])
        k.ts(dve, trif[:], maskf[:], -1.0 / 16.0, None, ALU.mult, ALU.bypass, [maskf], [trif])
        k.ts(dve, trib[:], maskb[:], -1.0 / 16.0, None, ALU.mult, ALU.bypass, [maskb], [trib])
        k.op(dve, lambda e: e.memset(ones[:], 1.0), [], [ones])

        def bcast_load(dst, src_row_ap):
            k.dma(dst[:], src_row_ap.partition_broadcast(128), writes=[dst])

        def mrow(l, r, j):
            return mrows[l, r:r + 1, j * D:(j + 1) * D]

        def load_mod(st, l, which, norm_w, gate=True):
            res = {}
            nm = k.sb(st, "nm", [128, D], F32)
            bcast_load(nm, norm_w[l:l + 1, :])
            for r in (0, 1):
                sg = k.sb(st, "sg%d" % r, [128, D], F32)
                sh = k.sb(st, "sh%d" % r, [128, D], F32)
                gt = k.sb(st, "gt%d" % r, [128, D], F32) if gate else None
                k.dma(sg[:], mrow(l, r, 3 * which + 1).partition_broadcast(128), reads=[mrows], writes=[sg])
                k.dma(sh[:], mrow(l, r, 3 * which + 0).partition_broadcast(128), reads=[mrows], writes=[sh])
                if gate:
                    k.dma(gt[:], mrow(l, r, 3 * which + 2).partition_broadcast(128), reads=[mrows], writes=[gt])
                k.stt(dve, sg[:], sg[:], 1.0, nm[:], ALU.add, ALU.mult, [sg, nm], [sg])
                res[r] = (sg, sh, gt)
            return res

        def norm_mod_T(i, xt, SG, SH, junk, ssb, hb, hT, psT, t32, dst=None):
            k.actf(junk[:], xt[:], AF.Square, [xt], [junk, ssb], accum_out=ssb[:, 0:1])
            k.rstd(ssb[:, 2:3], ssb[:, 1:2], ssb[:, 0:1], 1.0 / D, (ssb, ssb, ssb))
            k.stt(dve, t32[:], xt[:], ssb[:, 2:3], SG[:], ALU.mult, ALU.mult, [xt, ssb, SG], [t32])
            k.tt(dve, hb[:], t32[:], SH[:], ALU.add, [t32, SH], [hb])
            for kc in range(8):
                k.tr(psT[:, kc * 128:(kc + 1) * 128], hb[:, kc * 128:(kc + 1) * 128], identb[:], [hb, identb], [psT])
            if dst is None:
                k.cp(act, hT[:].rearrange("p a b -> p (a b)"), psT[:, 0:1024], [psT], [hT])
            else:
                k.cp(act, dst, psT[:, 0:1024].rearrange("p (a b) -> p a b", a=8), [psT], [hT])

        def cast_load(st_bufs, cnt, dst_ap, dst_buf, src_ap, shape_slice, engs=None):
            stg = st_bufs[cnt[0] % len(st_bufs)]
            cnt[0] += 1
            k.dma(shape_slice(stg), src_ap, writes=[stg])
            engs = engs or (pool, dve, act)
            k.cp(engs[cnt[0] % len(engs)], dst_ap, shape_slice(stg), [stg], [dst_buf])

        def phase0():
            with contextlib.ExitStack() as st:
                cT = k.sb(st, "cT", [128, 8, 2], F32)
                sT = k.sb(st, "sT", [128, 8, 2], F32)
                wst = [k.sb(st, "adaw%d" % i, [128, 3072], F32) for i in range(2)]
                bst = k.sb(st, "adab", [1, 6 * D], F32)
                mt = k.sb(st, "mt", [2, 6 * D], F32)
                pss = [k.ps(st, "psada%d" % i, [128, 512]) for i in range(6)]
                k.dma(cT[:], c2T[:, :, :], writes=[cT])
                k.actf(sT[:], cT[:], AF.Silu, [cT], [sT])
                n = 0
                for l in range(2):
                    k.dma(bst[:], ada_b[l:l + 1, :], writes=[bst])
                    for half in range(2):
                        for kc in range(8):
                            w = wst[n % 2]
                            n += 1
                            k.dma(w[:], ada_w[l, kc * 128:(kc + 1) * 128, half * 3072:(half + 1) * 3072], writes=[w])
                            for j in range(6):
                                k.mm(pss[j][0:2, :], sT[:, kc, :], w[:, j * 512:(j + 1) * 512], kc == 0, False, [sT, w], [pss[j]])
                        for j in range(6):
                            c0 = half * 3072 + j * 512
                            k.mm(pss[j][0:2, :], ones[0:1, 0:2], bst[0:1, c0:c0 + 512], False, True, [ones, bst], [pss[j]])
                            k.cp(act, mt[0:2, c0:c0 + 512], pss[j][0:2, :], [pss[j]], [mt])
                    k.dma(mrows[l, :, :], mt[0:2, :], reads=[mt], writes=[mrows], q="pool")
            k.barrier()

        def phase_e1(qs, ks, vs):
            with contextlib.ExitStack() as st:
                mod = load_mod(st, 0, 0, norm_mix, gate=False)
                wb = k.sb(st, "winb", [128, 8, 2336], BF16)
                with contextlib.ExitStack() as st2:
                    stg = [k.sb(st2, "wstg%d" % i, [128, 2336], F32) for i in range(2)]
                    cnt = [0]
                    for kc in range(8):
                        cast_load(stg, cnt, wb[:, kc, :], wb, even_w_in[kc * 128:(kc + 1) * 128, :], lambda s: s[:, :])
                    k.barrier()
                wup = k.sb(st, "wup", [33, 512], F32)
                k.dma(wup[:], gate_wup[:, :], writes=[wup])
                gqk = k.sb(st, "gqk", [128, 640], F32)
                bcast_load(gqk, qk_gain[0:1, :])
                k.ts(dve, gqk[:, 0:512], gqk[:, 0:512], 0.125, None, ALU.mult, None, [gqk], [gqk])
                cosf = k.sb(st, "cosf", [128, 32, 64], F32)
                sinf = k.sb(st, "sinf", [128, 32, 64], F32)
                k.dma(cosf[:], c_cos[:, :, :], writes=[cosf])
                k.dma(sinf[:], c_sin[:, :, :], writes=[sinf])
                R = 2
                xts = [k.sb(st, "xt%d" % i, [128, D], F32) for i in range(4)]
                junk = k.sb(st, "junk", [128, D], BF16)
                t32 = k.sb(st, "t32", [128, D], F32)
                ssb = [k.sb(st, "ssb%d" % i, [128, 4], F32) for i in range(4)]
                hb = [k.sb(st, "hb%d" % i, [128, D], BF16) for i in range(4)]
                hT = [k.sb(st, "hT%d" % i, [128, 8, 128], BF16) for i in range(3)]
                qk = [k.sb(st, "qk%d" % i, [128, 10, 64], F32) for i in range(R)]
                sq = k.sb(st, "sq", [128, 10, 64], F32)
                st10 = k.sb(st, "st10", [128, 32], F32)
                qn = k.sb(st, "qn", [128, 10, 64], F32)
                t1 = k.sb(st, "t1", [128, 10, 64], F32)
                t2 = k.sb(st, "t2", [128, 10, 64], F32)
                qrs = [k.sb(st, "qr%d" % i, [128, 10, 64], BF16) for i in range(R)]
                kds = [k.sb(st, "kd%d" % i, [128, 2, 2, 64], BF16) for i in range(R)]
                vbt = [k.sb(st, "vbt%d" % i, [128, 512], BF16) for i in range(R)]
                gbt = [k.sb(st, "gbt%d" % i, [128, 512], BF16) for i in range(R)]
                fT = [k.sb(st, "fT%d" % i, [128, 4, 128], BF16) for i in range(R)]
                lrT = [k.sb(st, "lrT%d" % i, [33, 128], F32) for i in range(R)]
                ez = k.sb(st, "ez", [128, 512], F32)
                spt = [k.sb(st, "spt%d" % i, [128, 512], F32) for i in range(R)]
                psT = k.ps(st, "psT", [128, 1024], BF16)
                psT2 = k.ps(st, "psT2e", [128, 1024], BF16)
                psQ = k.ps(st, "psQ", [128, 512])
                psK = k.ps(st, "psK", [128, 512])
                psV = k.ps(st, "psV", [128, 512])
                psG = k.ps(st, "psG", [128, 512])
                psF = k.ps(st, "psF", [128, 512])
                psZ = k.ps(st, "psZ", [128, 512])
                k.op(dve, lambda e: e.memset(vs[:], 1.0), [], [vs])
                for v in lrT:
                    k.op(dve, lambda e, v=v: e.memset(v[:], 1.0), [], [v])
                def stage_a0a(i):
                    r4 = i % 4
                    is_ctx = i < 2
                    tok0 = i * 128
                    SG, SH, _ = mod[1 if is_ctx else 0]
                    xt, sb_, hbi = xts[r4], ssb[r4], hb[r4]
                    k.dma(xt[:], xc[tok0:tok0 + 128, :], writes=[xt])
                    k.actf(junk[:], xt[:], AF.Square, [xt], [junk, sb_], accum_out=sb_[:, 0:1])
                    k.rstd(sb_[:, 2:3], sb_[:, 1:2], sb_[:, 0:1], 1.0 / D, (sb_, sb_, sb_))
                    k.stt(dve, t32[:], xt[:], sb_[:, 2:3], SG[:], ALU.mult, ALU.mult, [xt, sb_, SG], [t32])
                    k.tt(dve, hbi[:], t32[:], SH[:], ALU.add, [t32, SH], [hbi])

                def stage_a0b(i):
                    hbi, hTi = hb[i % 4], hT[i % 3]
                    for kc in range(8):
                        k.tr(psT[:, kc * 128:(kc + 1) * 128], hbi[:, kc * 128:(kc + 1) * 128], identb[:], [hbi, identb], [psT])
                    k.cp(act, hTi[:].rearrange("p a b -> p (a b)"), psT[:, 0:1024], [psT], [hTi])

                def stage_a1(i):
                    r = i % R
                    tok0 = i * 128
                    h = hT[i % 3]
                    for (psb, c0, n) in ((psQ, 0, 512), (psK, 512, 256), (psV, 1280, 512), (psG, 1792, 512)):
                        for kc in range(8):
                            k.mm(psb[:, 0:n], h[:, kc, :], wb[:, kc, c0:c0 + n], kc == 0, kc == 7, [h, wb], [psb])
                    for j, c0 in enumerate((768, 896, 1024, 1152)):
                        for kc in range(8):
                            k.mm(psF[:, j * 128:(j + 1) * 128], wb[:, kc, c0:c0 + 128], h[:, kc, :], kc == 0, kc == 7, [h, wb], [psF])
                    for kc in range(8):
                        k.mm(psK[0:32, 256:384], wb[:, kc, 2304:2336], h[:, kc, :], kc == 0, kc == 7, [h, wb], [psK])
                    k.cp(act, qk[r][:, 0:8, :].rearrange("p a b -> p (a b)"), psQ[:, 0:512], [psQ], [qk[r]])
                    k.cp(act, qk[r][:, 8:10, :].rearrange("p a b -> p (a b)"), psK[:, 0:128], [psK], [qk[r]])
                    k.cp(act, vs[:, i, :].rearrange("p (a b) -> p a b", a=2)[:, :, 0:64], psK[:, 128:256].rearrange("p (a b) -> p a b", a=2), [psK], [vs])
                    k.cp(act, vbt[r][:], psV[:, :], [psV], [vbt[r]])
                    k.dma(vB_d[tok0:tok0 + 128, :], vbt[r][:], reads=[vbt[r]], writes=[vB_d], q="pool")
                    k.cp(act, gbt[r][:], psG[:, :], [psG], [gbt[r]])
                    k.dma(gB_d[tok0:tok0 + 128, :], gbt[r][:], reads=[gbt[r]], writes=[gB_d], q="pool")
                    k.actf(fT[r][:, 0:2, :].rearrange("p a b -> p (a b)"), psF[:, 0:256], AF.Copy, [psF], [fT[r]], scale=0.125)
                    k.cp(act, fT[r][:, 2:4, :].rearrange("p a b -> p (a b)"), psF[:, 256:512], [psF], [fT[r]])
                    k.dma(qBT_d[:, :, tok0:tok0 + 128].rearrange("j p t -> p j t"), fT[r][:, 0:2, :], reads=[fT[r]], writes=[qBT_d], q="pool")
                    k.dma(kBT_d[:, :, tok0:tok0 + 128].rearrange("j p t -> p j t"), fT[r][:, 2:4, :], reads=[fT[r]], writes=[kBT_d], q="pool")
                    k.cp(act, lrT[r][0:32, :], psK[0:32, 256:384], [psK], [lrT[r]])
                def stage_b0(i):
                    r = i % R
                    is_ctx = i < 2
                    tok0 = i * 128
                    qr, kd = qrs[r], kds[r]
                    k.tt(dve, sq[:], qk[r][:], qk[r][:], ALU.mult, [qk[r]], [sq])
                    k.op(dve, lambda e: e.tensor_reduce(out=st10[:, 0:10], in_=sq[:], axis=AX.X, op=ALU.add), [sq], [st10])
                    k.rstd(st10[:, 20:30], st10[:, 10:20], st10[:, 0:10], 1.0 / 64, (st10, st10, st10))
                    k.tt(dve, qn[:], qk[r][:], st10[:, 20:30].unsqueeze(2).to_broadcast([128, 10, 64]), ALU.mult, [qk[r], st10], [qn])
                    k.tt(dve, qn[:], qn[:], gqk[:].rearrange("p (a b) -> p a b", a=10), ALU.mult, [qn, gqk], [qn])
                    if is_ctx:
                        k.cp(dve, qr[:], qn[:], [qn], [qr])
                    else:
                        li = i - 2
                        k.tt(dve, t1[:], qn[:], cosf[:, li, :].unsqueeze(1).to_broadcast([128, 10, 64]), ALU.mult, [qn, cosf], [t1])
                        qv = qn[:].rearrange("p n (a h f) -> p n a h f", a=2, h=2)
                        tv = t2[:].rearrange("p n (a h f) -> p n a h f", a=2, h=2)
                        sv = sinf[:, li, :].rearrange("p (a h f) -> p a h f", a=2, h=2)
                        for half in range(2):
                            k.tt(dve, tv[:, :, :, half, :], qv[:, :, :, 1 - half, :],
                                 sv[:, :, half, :].unsqueeze(1).to_broadcast([128, 10, 2, 16]), ALU.mult, [qn, sinf], [t2])
                        k.tt(dve, qr[:], t1[:], t2[:], ALU.add, [t1, t2], [qr])
                    k.cp(dve, kd[:], qr[:, 8:10, :].unsqueeze(2).to_broadcast([128, 2, 2, 64]), [qr], [kd])

                def stage_b1(i):
                    r = i % R
                    tok0 = i * 128
                    qr, kd = qrs[r], kds[r]
                    for j in range(4):
                        k.tr(psT2[:, j * 128:(j + 1) * 128], qr[:, 2 * j:2 * j + 2, :].rearrange("p a b -> p (a b)"), identb[:], [qr, identb], [psT2])
                    for g in range(2):
                        k.tr(psT2[:, (4 + g) * 128:(5 + g) * 128], kd[:, g, :, :].rearrange("p a b -> p (a b)"), identb[:], [kd, identb], [psT2])
                    k.cp(act, qs[:, :, tok0:tok0 + 128], psT2[:, 0:512].rearrange("p (a b) -> p a b", a=4), [psT2], [qs])
                    k.cp(act, ks[:, :, tok0:tok0 + 128], psT2[:, 512:768].rearrange("p (a b) -> p a b", a=2), [psT2], [ks])
                    k.mm(psZ[:, :], lrT[r][0:33, :], wup[0:33, :], True, True, [lrT[r], wup], [psZ])
                    k.actf(ez[:], psZ[:, :], AF.Exp, [psZ], [ez], scale=-1.0)
                    k.actf(spt[r][:], ez[:], AF.Ln, [ez], [spt[r]], bias=1.0)
                    k.dma(sp_d[tok0:tok0 + 128, :], spt[r][:], reads=[spt[r]], writes=[sp_d], q="pool")
                for t_ in range(3):
                    stage_a0a(t_)
                stage_a0b(0)
                stage_a0b(1)
                stage_a1(0)
                for i in range(NT):
                    if i + 3 < NT:
                        stage_a0a(i + 3)
                    stage_b0(i)
                    if i + 2 < NT:
                        stage_a0b(i + 2)
                    if i + 1 < NT:
                        stage_a1(i + 1)
                    stage_b1(i)
            k.barrier()

        def phase_att(qs, ks, vs):
            with contextlib.ExitStack() as st:
                NS = 2
                psS = [k.ps(st, "psS%d" % i, [128, 1024]) for i in range(NS)]
                NP = 12
                pts = [k.sb(st, "pt%d" % i, [128, 1024], BF16) for i in range(NP)]
                psO = [k.ps(st, "psO%d" % i, [128, 512]) for i in range(3)]
                psB = k.ps(st, "psB", [128, 512])
                rec = [k.sb(st, "rec%d" % i, [65, 512], F32) for i in range(3)]
                bc = [k.sb(st, "bc%d" % i, [64, 512], F32) for i in range(2)]
                ob = [k.sb(st, "ob%d" % i, [64, 512], BF16) for i in range(2)]
                blocks = [(0, 256, [0, 1])] + [(256 + 512 * j, 512, list(range(NT))) for j in range(8)]
                its = []
                pi = 0
                for (tok0, nq, kcs) in blocks:
                    for p in range(4):
                        for idx, kc in enumerate(kcs):
                            its.append((tok0, nq, p, idx, kc, len(kcs), pi))
                        pi += 1
                sidx = [0]
                sbank = {}

                def emit_S(j):
                    tok0, nq, p, idx, kc, nk, pid = its[j]
                    g = p // 2
                    pS = psS[sidx[0] % NS]
                    sbank[j] = pS
                    sidx[0] += 1
                    for hf in range(2):
                        pb = hf * 64
                        k.mm(pS[:, hf * 512:hf * 512 + nq], ks[pb:pb + 64, g, kc * 128:(kc + 1) * 128], qs[pb:pb + 64, p, tok0:tok0 + nq],
                             True, True, [ks, qs], [pS])

                def emit_exp_pv(j, hfs):
                    tok0, nq, p, idx, kc, nk, pid = its[j]
                    g = p // 2
                    pt = pts[j % NP]
                    if 0 in hfs or hfs == (0, 1):
                        pS = sbank.pop(j)
                        k.actf(pt[:].rearrange("p (a n) -> p a n", a=2)[:, :, 0:nq], pS[:].rearrange("p (a n) -> p a n", a=2)[:, :, 0:nq],
                               AF.Exp, [pS], [pt], bias=-8.0)
                    for hf in hfs:
                        pO = psO[(2 * pid + hf) % 3]
                        k.mm(pO[0:65, 0:nq], vs[:, kc, g * 65:(g + 1) * 65], pt[:, hf * 512:hf * 512 + nq], idx == 0, idx == nk - 1, [vs, pt], [pO])

                nrm = [0]

                def norm_stage(stage, info):
                    tok0, nq, h, ob_i, pO, rc, pB = info
                    b_, o_ = bc[ob_i], ob[ob_i]
                    if stage == 0:
                        k.op(dve, lambda e: e.reciprocal(out=rc[64:65, 0:nq], in_=pO[64:65, 0:nq]), [pO], [rc])
                    elif stage == 1:
                        pB[0] = psB
                        k.mm(pB[0][0:64, 0:nq], ones[64:65, 0:64], rc[64:65, 0:nq], True, True, [ones, rc], [pB[0]])
                    elif stage == 2:
                        k.cp(act, b_[0:64, 0:nq], pB[0][0:64, 0:nq], [pB[0]], [b_])
                    else:
                        k.tt(dve, o_[0:64, 0:nq], pO[0:64, 0:nq], b_[0:64, 0:nq], ALU.mult, [pO, b_], [o_])
                        k.dma(mixAT_d[h, :, tok0:tok0 + nq], o_[0:64, 0:nq], reads=[o_], writes=[mixAT_d], q="pool")

                sched = {}
                n_it = len(its)
                LOOK = 1
                DL = 8
                held = []
                long_pair = {}
                for it_ in its:
                    long_pair[it_[6]] = it_[5] >= 16
                for j in range(n_it + LOOK + 24):
                    if j < n_it:
                        emit_S(j)
                    jj = j - LOOK
                    if 0 <= jj < n_it:
                        tok0, nq, p, idx, kc, nk, pid = its[jj]
                        if long_pair[pid] and long_pair.get(pid - 1, False) and idx < DL:
                            emit_exp_pv(jj, (0,))
                            held.append(jj)
                        else:
                            for hj in held:
                                emit_exp_pv(hj, (1,))
                            held = []
                            emit_exp_pv(jj, (0, 1))
                        if idx == nk - 1:
                            for hf in range(2):
                                info = (tok0, nq, 2 * p + hf, nrm[0] % 2, psO[(2 * pid + hf) % 3], rec[(2 * pid + hf) % 3], [None])
                                nrm[0] += 1
                                for sgi in range(4):
                                    when = (j + 1 + 4 * hf + sgi) if nk >= 16 else j
                                    sched.setdefault(when, []).append((sgi, info))
                    for (sgi, info) in sched.pop(j, []):
                        norm_stage(sgi, info)
                assert not sched
            k.barrier()

        def phase_gla():
            with contextlib.ExitStack() as st:
                R = 2
                qbt = [k.sb(st, "qbt%d" % i, [128, 2, 128], BF16) for i in range(3)]
                kbt = [k.sb(st, "kbt%d" % i, [128, 2, 128], BF16) for i in range(3)]
                vb = [k.sb(st, "vb%d" % i, [128, 512], BF16) for i in range(3)]
                spd = [k.sb(st, "spd%d" % i, [128, 256], F32) for i in range(3)]
                gb = [k.sb(st, "gb%d" % i, [128, 512], BF16) for i in range(3)]
                of = [k.sb(st, "of%d" % i, [128, 512], F32) for i in range(3)]
                Eq = [k.sb(st, "Eq%d" % i, [128, 256], F32) for i in range(R)]
                Eks = [k.sb(st, "Ek%d" % i, [128, 256], F32) for i in range(R)]
                qt = [k.sb(st, "qt%d" % i, [128, 2, 128], BF16) for i in range(R)]
                kt = [k.sb(st, "kt%d" % i, [128, 2, 128], BF16) for i in range(R)]
                ktok = [k.sb(st, "ktok%d" % i, [128, 256], BF16) for i in range(R)]
                Am = [k.sb(st, "Am%d" % i, [128, 4, 128], BF16) for i in range(R)]
                S32 = [k.sb(st, "S32_%d" % i, [128, 256], F32) for i in range(2)]
                Stmp = [k.sb(st, "Stmp%d" % i, [128, 256], F32) for i in range(2)]
                Sb = [k.sb(st, "Sb%d" % i, [128, 256], BF16) for i in range(2)]
                ofs = [k.sb(st, "ofs%d" % i, [128, 512], F32) for i in range(R)]
                o32s = [k.sb(st, "o32_%d" % i, [128, 4, 128], F32) for i in range(2)]
                osq = k.sb(st, "osq", [128, 4, 128], F32)
                st4s = [k.sb(st, "st4_%d" % i, [128, 16], F32) for i in range(2)]
                sgl = k.sb(st, "sgl", [128, 512], F32)
                obfs = [k.sb(st, "obf%d" % i, [128, 4, 128], BF16) for i in range(2)]
                mbt = [k.sb(st, "mbt%d" % i, [128, 4, 128], BF16) for i in range(R)]
                og = k.sb(st, "og", [128, 512], F32)
                bcast_load(og, gla_og[0:1, :])
                psC = k.ps(st, "psC", [128, 512])
                psT = k.ps(st, "psTg", [128, 1024], BF16)
                psAx = [k.ps(st, "psA%d" % i, [128, 512]) for i in range(2)]
                psOx = [k.ps(st, "psOg%d" % i, [128, 512]) for i in range(2)]
                psS = k.ps(st, "psSg", [128, 512])
                psT2 = k.ps(st, "psT2", [128, 1024], BF16)
                n = 0
                for d in range(2):
                    order = list(range(NT)) if d == 0 else [1, 0] + list(range(NT - 1, 1, -1))
                    tri = trif if d == 0 else trib
                    msk = maskf if d == 0 else maskb
                    tl = 127 if d == 0 else 0
                    for p in range(2):
                        k.op(dve, lambda e, p=p: e.memset(S32[p][:], 0.0), [], [S32[p]])
                        k.op(dve, lambda e, p=p: e.memset(Sb[p][:], 0.0), [], [Sb[p]])
                    def front(n, i):
                        r = n % R
                        tok0 = i * 128
                        k.dma(qbt[n % 3][:], qBT_d[:, :, tok0:tok0 + 128].rearrange("j p t -> p j t"), reads=[qBT_d], writes=[qbt[n % 3]])
                        k.dma(kbt[n % 3][:], kBT_d[:, :, tok0:tok0 + 128].rearrange("j p t -> p j t"), reads=[kBT_d], writes=[kbt[n % 3]])
                        k.dma(vb[n % 3][:], vB_d[tok0:tok0 + 128, :], reads=[vB_d], writes=[vb[n % 3]])
                        k.dma(spd[n % 3][:], sp_d[tok0:tok0 + 128, d * 256:(d + 1) * 256], reads=[sp_d], writes=[spd[n % 3]])
                        if d == 1:
                            k.dma(gb[n % 3][:], gB_d[tok0:tok0 + 128, :], reads=[gB_d], writes=[gb[n % 3]])
                            k.dma(of[n % 3][:], oF_d[tok0:tok0 + 128, :], reads=[oF_d], writes=[of[n % 3]])
                        for p in range(2):
                            k.mm(psC[:, p * 128:(p + 1) * 128], spd[n % 3][:, p * 128:(p + 1) * 128], tri[:], True, True, [spd[n % 3], tri], [psC])
                        k.actf(Eq[r][:], psC[:, 0:256], AF.Exp, [psC], [Eq[r]])
                        Ek = Eks[r]
                        k.actf(Ek[:], psC[:, 0:256], AF.Exp, [psC], [Ek], scale=-1.0)
                        k.tt(dve, qt[r][:].rearrange("p a b -> p (a b)"), qbt[n % 3][:].rearrange("p a b -> p (a b)"), Eq[r][:], ALU.mult, [qbt[n % 3], Eq[r]], [qt[r]])
                        k.tt(dve, kt[r][:].rearrange("p a b -> p (a b)"), kbt[n % 3][:].rearrange("p a b -> p (a b)"), Ek[:], ALU.mult, [kbt[n % 3], Ek], [kt[r]])
                        for p in range(2):
                            k.tr(psT[:, p * 128:(p + 1) * 128], kt[r][:, p, :], identb[:], [kt[r], identb], [psT])
                        k.cp(act, ktok[r][:], psT[:, 0:256], [psT], [ktok[r]])
                        for h in range(4):
                            p, pb = h // 2, (h % 2) * 64
                            k.mm(psAx[h % 2][:, p * 128:(p + 1) * 128], kt[r][pb:pb + 64, p, :], qt[r][pb:pb + 64, p, :], True, True,
                                 [kt[r], qt[r]], [psAx[h % 2]])
                        Amv = Am[r][:].rearrange("p (a h) b -> p a h b", h=2)
                        for hf in range(2):
                            k.tt(dve, Amv[:, :, hf, :], psAx[hf][:, 0:256].rearrange("p (a b) -> p a b", a=2),
                                 msk[:].unsqueeze(1).to_broadcast([128, 2, 128]), ALU.mult, [psAx[hf], msk], [Am[r]])

                    def back(n, i):
                        r = n % R
                        tok0 = i * 128
                        for h in range(4):
                            p, hf = h // 2, h % 2
                            pb = hf * 64
                            k.mm(psOx[hf][:, p * 128:(p + 1) * 128], Am[r][:, h, :], vb[n % 3][:, h * 128:(h + 1) * 128], True, False,
                                 [Am[r], vb[n % 3]], [psOx[hf]])
                            k.mm(psOx[hf][:, p * 128:(p + 1) * 128], qt[r][pb:pb + 64, p, :], Sb[p][pb:pb + 64, hf * 128:(hf + 1) * 128], False, True,
                                 [qt[r], Sb[p]], [psOx[hf]])
                        for p in range(2):
                            k.mm(psS[:, p * 256:(p + 1) * 256], ktok[r][:, p * 128:(p + 1) * 128], vb[n % 3][:, p * 256:(p + 1) * 256], True, True,
                                 [ktok[r], vb[n % 3]], [psS])
                        for p in range(2):
                            el = Eq[r][:, p * 128 + tl:p * 128 + tl + 1]
                            k.ts(dve, Stmp[p][:], S32[p][:], el, None, ALU.mult, None, [S32[p], Eq[r]], [Stmp[p]])
                            k.stt(dve, S32[p][:], psS[:, p * 256:(p + 1) * 256], el, Stmp[p][:], ALU.mult, ALU.add, [psS, Eq[r], Stmp[p]], [S32[p]])
                            k.cp(act, Sb[p][:], S32[p][:], [S32[p]], [Sb[p]])
                        if d == 0:
                            ofv = ofs[r][:].rearrange("p (a h b) -> p a h b", a=2, h=2)
                            for hf in range(2):
                                k.cp(act, ofv[:, :, hf, :], psOx[hf][:, 0:256].rearrange("p (a b) -> p a b", a=2), [psOx[hf]], [ofs[r]])
                            k.dma(oF_d[tok0:tok0 + 128, :], ofs[r][:], reads=[ofs[r]], writes=[oF_d], q="pool")
                        else:
                            o32 = o32s[n % 2]
                            o4 = o32[:].rearrange("p (a h) b -> p a h b", h=2)
                            of4 = of[n % 3][:].rearrange("p (a h b) -> p a h b", a=2, h=2)
                            for hf in range(2):
                                k.tt(dve, o4[:, :, hf, :], psOx[hf][:, 0:256].rearrange("p (a b) -> p a b", a=2), of4[:, :, hf, :], ALU.add,
                                     [psOx[hf], of[n % 3]], [o32])

                    def out_a(n):
                        o32, obf, st4, gbn = o32s[n % 2], obfs[n % 2], st4s[n % 2], gb[n % 3]
                        o2 = o32[:].rearrange("p a b -> p (a b)")
                        k.tt(pool, osq[:], o32[:], o32[:], ALU.mult, [o32], [osq])
                        k.op(dve, lambda e: e.tensor_reduce(out=st4[:, 0:4], in_=osq[:], axis=AX.X, op=ALU.add), [osq], [st4])
                        k.rstd(st4[:, 8:12], st4[:, 4:8], st4[:, 0:4], 1.0 / 128, (st4, st4, st4))
                        k.actf(sgl[:], gbn[:], AF.Silu, [gbn], [sgl])
                        k.tt(dve, o32[:], o32[:], st4[:, 8:12].unsqueeze(2).to_broadcast([128, 4, 128]), ALU.mult, [o32, st4], [o32])
                        k.tt(pool, o2, o2, og[:], ALU.mult, [o32, og], [o32])
                        k.tt(pool, obf[:].rearrange("p a b -> p (a b)"), o2, sgl[:], ALU.mult, [o32, sgl], [obf])

                    def out_b(n, i):
                        obf, r = obfs[n % 2], n % R
                        tok0 = i * 128
                        for h in range(4):
                            k.tr(psT2[:, h * 128:(h + 1) * 128], obf[:, h, :], identb[:], [obf, identb], [psT2])
                        k.cp(act, mbt[r][:].rearrange("p a b -> p (a b)"), psT2[:, 0:512], [psT2], [mbt[r]])
                        k.dma(mixBT_d[:, :, tok0:tok0 + 128].rearrange("h p t -> p h t"), mbt[r][:], reads=[mbt[r]], writes=[mixBT_d], q="pool")

                    n0 = n
                    front(n0, order[0])
                    if len(order) > 1:
                        front(n0 + 1, order[1])
                    for jn, i in enumerate(order):
                        back(n0 + jn, i)
                        if d == 1:
                            if jn >= 1:
                                out_a(n0 + jn - 1)
                            if jn >= 2:
                                out_b(n0 + jn - 2, order[jn - 2])
                        if jn + 2 < len(order):
                            front(n0 + jn + 2, order[jn + 2])
                    if d == 1:
                        L_ = len(order)
                        out_a(n0 + L_ - 1)
                        out_b(n0 + L_ - 2, order[L_ - 2])
                        out_b(n0 + L_ - 1, order[L_ - 1])
                    n = n0 + len(order)
                    k.barrier()
        def phase_post(l, wup=None):
            with contextlib.ExitStack() as st:
                mod1 = {}
                for r_ in ((0, 1) if l == 0 else (0,)):
                    gt_ = k.sb(st, "gt1_%d" % r_, [128, D], F32)
                    k.dma(gt_[:], mrow(l, r_, 2).partition_broadcast(128), reads=[mrows], writes=[gt_])
                    mod1[r_] = (None, None, gt_)
                stg = [k.sb(st, "wstg%d" % i, [128, 2048], F32) for i in range(3)]
                cnt = [0]
                pcnt = [0]
                pieces = [(kc, c0, n_) for kc in range(8) for (c0, n_) in ((0, 2048), (2048, 2048), (4096, 1536))]
                if l == 0:
                    woA = k.sb(st, "woA", [128, 4, D], BF16)
                    woB = k.sb(st, "woB", [128, 4, D], BF16)
                    for j in range(0, 4, 2):
                        cast_load(stg, cnt, woA[:, j:j + 2, :], woA,
                                  even_w_out[j * 128:(j + 2) * 128, :].rearrange("(a p) n -> p a n", p=128),
                                  lambda s: s[:, :].rearrange("p (a n) -> p a n", a=2))
                    for j in range(0, 4, 2):
                        cast_load(stg, cnt, woB[:, j:j + 2, :], woB,
                                  even_w_out[512 + j * 128:512 + (j + 2) * 128, :].rearrange("(a p) n -> p a n", p=128),
                                  lambda s: s[:, :].rearrange("p (a n) -> p a n", a=2))
                    tiles = list(range(NT))
                else:
                    woL = k.sb(st, "woL", [128, 10, D], BF16)
                    for j in range(0, 10, 2):
                        cast_load(stg, cnt, woL[:, j:j + 2, :], woL,
                                  lru_w_out[j * 128:(j + 2) * 128, :].rearrange("(a p) n -> p a n", p=128),
                                  lambda s: s[:, :].rearrange("p (a n) -> p a n", a=2))
                    tiles = list(range(2, NT))
                mod2 = {}
                nm2 = k.sb(st, "nm2", [128, D], F32)
                bcast_load(nm2, norm_ffn[l:l + 1, :])
                for r in ((0, 1) if l == 0 else (0,)):
                    sg = k.sb(st, "sg2_%d" % r, [128, D], F32)
                    sh = k.sb(st, "sh2_%d" % r, [128, D], F32)
                    k.dma(sg[:], mrow(l, r, 4).partition_broadcast(128), reads=[mrows], writes=[sg])
                    k.dma(sh[:], mrow(l, r, 3).partition_broadcast(128), reads=[mrows], writes=[sh])
                    k.stt(dve, sg[:], sg[:], 1.0, nm2[:], ALU.add, ALU.mult, [sg, nm2], [sg])
                    mod2[r] = (sg, sh)
                R = 2
                xts = [k.sb(st, "xt%d" % i, [128, D], F32) for i in range(R)]
                x1t = [k.sb(st, "x1t%d" % i, [128, D], F32) for i in range(R)]
                junk = k.sb(st, "junk", [128, D], BF16)
                t32 = k.sb(st, "t32", [128, D], F32)
                ssb = [k.sb(st, "ssb%d" % i, [128, 4], F32) for i in range(R)]
                hb = [k.sb(st, "hb%d" % i, [128, D], BF16) for i in range(R)]
                hT = [k.sb(st, "hT%d" % i, [128, 8, 128], BF16) for i in range(R)]
                zt = k.sb(st, "zt", [128, 8, 128], BF16)
                psT = k.ps(st, "psTp", [128, 1024], BF16)
                psY = [k.ps(st, "psY%d" % i, [128, 512]) for i in range(4)]
                if l == 0:
                    ma = [k.sb(st, "ma%d" % i, [128, 4, 128], BF16) for i in range(R)]
                    mb = [k.sb(st, "mb%d" % i, [128, 4, 128], BF16) for i in range(R)]
                else:
                    mr = [k.sb(st, "mr%d" % i, [128, 10, 128], BF16) for i in range(R)]
                k.op(dve, lambda e: e.memset(zt[:], 0.0), [], [zt])
                h2v = h2T_d[:, :, :].rearrange("k p t -> p k t")
                k.dma(h2v[:, :, 0:1], zt[:, :, 0:1], reads=[zt], writes=[h2T_d], q="pool", allow_slow_non_contiguous=True)
                k.dma(h2v[:, :, 257:259], zt[:, :, 0:2], reads=[zt], writes=[h2T_d], q="pool")
                for q4 in range(4):
                    k.dma(h2v[:, :, LATC + T + 128 * q4:LATC + T + 128 * (q4 + 1)], zt[:, :, 0:128], reads=[zt], writes=[h2T_d], q="pool")
                def post_a(n, i):
                    r = n % R
                    is_ctx = i < 2
                    rr = 1 if is_ctx else 0
                    tok0 = i * 128
                    xt = xts[r]
                    src = xc if l == 0 else xc2_d
                    k.dma(xt[:], src[tok0:tok0 + 128, :], reads=[src], writes=[xt])
                    if l == 0:
                        k.dma(ma[r][:], mixAT_d[:, :, tok0:tok0 + 128].rearrange("(j q) p t -> (q p) j t", q=2), reads=[mixAT_d], writes=[ma[r]])
                        k.dma(mb[r][:], mixBT_d[:, :, tok0:tok0 + 128].rearrange("h p t -> p h t"), reads=[mixBT_d], writes=[mb[r]])
                        chunks = [(ma[r][:, j, :], woA, j, ma[r]) for j in range(4)] + [(mb[r][:, j, :], woB, j, mb[r]) for j in range(4)]
                    else:
                        lt0 = tok0 - C
                        k.dma(mr[r][:], rgT_d[:, :, lt0:lt0 + 128].rearrange("h p t -> p h t"), reads=[rgT_d], writes=[mr[r]])
                        chunks = [(mr[r][:, j, :], woL, j, mr[r]) for j in range(10)]
                    GATE = mod1[rr][2]
                    for nh in range(2):
                        pY = psY[(2 * n + nh) % 4]
                        for ci, (lh, wbuf, wi, lbuf) in enumerate(chunks):
                            kk = lh.shape[0]
                            k.mm(pY[:, :], lh, wbuf[0:kk, wi, nh * 512:(nh + 1) * 512], ci == 0, ci == len(chunks) - 1, [lbuf, wbuf], [pY])
                        sl = slice(nh * 512, (nh + 1) * 512)
                        k.tt(dve, x1t[r][:, sl], pY[:, :], GATE[:, sl], ALU.mult, [pY, GATE], [x1t[r]])
                    k.tt(dve, x1t[r][:], x1t[r][:], xt[:], ALU.add, [x1t[r], xt], [x1t[r]])
                    k.dma(x1_d[tok0:tok0 + 128, :], x1t[r][:], reads=[x1t[r]], writes=[x1_d], q="pool")

                def post_b(n, i):
                    r = n % R
                    rr = 1 if i < 2 else 0
                    SG2, SH2 = mod2[rr]
                    xt_, sb_, hbi = x1t[r], ssb[r], hb[r]
                    k.actf(junk[:], xt_[:], AF.Square, [xt_], [junk, sb_], accum_out=sb_[:, 0:1])
                    k.rstd(sb_[:, 2:3], sb_[:, 1:2], sb_[:, 0:1], 1.0 / D, (sb_, sb_, sb_))
                    k.stt(dve, t32[:], xt_[:], sb_[:, 2:3], SG2[:], ALU.mult, ALU.mult, [xt_, sb_, SG2], [t32])
                    k.tt(dve, hbi[:], t32[:], SH2[:], ALU.add, [t32, SH2], [hbi])

                def post_c(n, i):
                    r = n % R
                    is_ctx = i < 2
                    tok0 = i * 128
                    hbi, hTi = hb[r], hT[r]
                    for kc in range(8):
                        k.tr(psT[:, kc * 128:(kc + 1) * 128], hbi[:, kc * 128:(kc + 1) * 128], identb[:], [hbi, identb], [psT])
                    k.cp(act, hTi[:].rearrange("p a b -> p (a b)"), psT[:, 0:1024], [psT], [hTi])
                    col0 = (1 + tok0) if is_ctx else (LATC + tok0 - C)
                    k.dma(h2v[:, :, col0:col0 + 128], hTi[:], reads=[hTi], writes=[h2T_d], q="pool")

                post_a(0, tiles[0])
                for n, i in enumerate(tiles):
                    if n + 1 < len(tiles):
                        post_a(n + 1, tiles[n + 1])
                    post_b(n, i)
                    if n >= 1:
                        post_c(n - 1, tiles[n - 1])
                    if wup is not None and n < len(pieces):
                        kc_, c0_, n_ = pieces[n]
                        cast_load(stg, pcnt, wup[:, kc_, c0_:c0_ + n_], wup, ffn_w_up[l, kc_ * 128:(kc_ + 1) * 128, c0_:c0_ + n_],
                                  lambda s_, n_=n_: s_[:, 0:n_], engs=(pool, act))
                post_c(len(tiles) - 1, tiles[-1])
            k.barrier()

        def phase_ffn(l, wup):
            last = l == 1
            with contextlib.ExitStack() as st:
                wdn = k.sb(st, "wdnb", [128, 22, D], BF16)
                with contextlib.ExitStack() as st2:
                    stg = [k.sb(st2, "wstg%d" % i, [128, 2048], F32) for i in range(3)]
                    cnt = [0]
                    for j in range(0, 22, 2):
                        cast_load(stg, cnt, wdn[:, j:j + 2, :], wdn,
                                  ffn_w_down[l, j * 128:(j + 2) * 128, :].rearrange("(a p) n -> p a n", p=128),
                                  lambda s: s[:, :].rearrange("p (a n) -> p a n", a=2))
                    k.barrier()
                cw = k.sb(st, "cw", [128, 44, 3], F32)
                k.dma(cw[:], ffn_conv[l, :, :, :], writes=[cw])
                gates = {}
                for r in ((0, 1) if l == 0 else (0,)):
                    gt = k.sb(st, "g2_%d" % r, [128, D], F32)
                    k.dma(gt[:], mrow(l, r, 5).partition_broadcast(128), reads=[mrows], writes=[gt])
                    gates[r] = gt
                if last:
                    fg = k.sb(st, "fg", [128, D], F32)
                    bcast_load(fg, final_gain[0:1, :])
                hTb = [k.sb(st, "hTb%d" % i, [128, 8, 384], BF16) for i in range(2)]
                NU = 4
                psU = [k.ps(st, "psU%d" % i, [128, 512]) for i in range(NU)]
                tcv = [k.sb(st, "tcv%d" % i, [128, 384], F32) for i in range(4)]
                sgt = [k.sb(st, "sgt%d" % i, [128, 384], F32) for i in range(2)]
                G = k.sb(st, "G", [128, 22, 384], BF16)
                psY = [k.ps(st, "psYf%d" % i, [128, 512]) for i in range(4)]
                x1t = [k.sb(st, "x1f%d" % i, [128, D], F32) for i in range(1)] * 2
                x2t = [k.sb(st, "x2f%d" % i, [128, D], F32) for i in range(1)] * 2
                if last:
                    x2g = [k.sb(st, "x2g%d" % i, [128, D], F32) for i in range(3)]
                    ssq = k.sb(st, "ssq", [128, 12], F32)
                    k.op(dve, lambda e: e.memset(ssq[:], 1.0), [], [ssq])
                junk = k.sb(st, "junkf", [128, D], BF16)
                ssb = [k.sb(st, "ssf%d" % i, [128, 4], F32) for i in range(2)]
                h2v = h2T_d[:, :, :].rearrange("k p t -> p k t")
                blocks = []
                if l == 0:
                    blocks.append((0, 256, 0, 1))
                s = 0
                while s < T:
                    nv = min(382, T - s)
                    blocks.append((LATC - 1 + s, nv, C + s, 0))
                    s += nv
                nu = 0
                ng = 0
                for bi, (col0, nv, tokb, rr) in enumerate(blocks):
                    N = nv + 2
                    hb_ = hTb[bi % 2]
                    k.dma(hb_[:, :, 0:N], h2v[:, :, col0:col0 + N], reads=[h2T_d], writes=[hb_])
                    for c in range(22):
                        tv = []
                        for which in range(2):
                            cc = c + 22 * which
                            pU = psU[nu % NU]
                            t = tcv[nu % 4]
                            nu += 1
                            for kc in range(8):
                                k.mm(pU[:, 0:N], wup[:, kc, cc * 128:(cc + 1) * 128], hb_[:, kc, 0:N], kc == 0, kc == 7, [wup, hb_], [pU])
                            k.actf(t[:, 0:nv], pU[:, 0:nv], AF.Copy, [pU, cw], [t], scale=cw[:, cc, 0:1])
                            k.stt(dve, t[:, 0:nv], pU[:, 1:nv + 1], cw[:, cc, 1:2], t[:, 0:nv], ALU.mult, ALU.add, [pU, cw, t], [t])
                            k.stt(dve, t[:, 0:nv], pU[:, 2:nv + 2], cw[:, cc, 2:3], t[:, 0:nv], ALU.mult, ALU.add, [pU, cw, t], [t])
                            tv.append(t)
                        sg = sgt[c % 2]
                        k.actf(sg[:, 0:nv], tv[0][:, 0:nv], AF.Silu, [tv[0]], [sg])
                        k.tt(pool, G[:, c, 0:nv], sg[:, 0:nv], tv[1][:, 0:nv], ALU.mult, [sg, tv[1]], [G])
                    ngrp = (nv + 127) // 128
                    for m in range(ngrp):
                        nt = min(128, nv - 128 * m)
                        tk = tokb + 128 * m
                        r = ng % 2
                        ng += 1
                        k.dma(x1t[r][0:nt, :], x1_d[tk:tk + nt, :], reads=[x1_d], writes=[x1t[r]])
                        for nh in range(2):
                            pY = psY[(2 * ng + nh) % 4]
                            for c in range(22):
                                k.mm(pY[0:nt, :], G[:, c, 128 * m:128 * m + nt], wdn[:, c, nh * 512:(nh + 1) * 512], c == 0, c == 21, [G, wdn], [pY])
                            sl = slice(nh * 512, (nh + 1) * 512)
                            xo = x2g[m] if last else x2t[r]
                            k.tt(dve, xo[0:nt, sl], pY[0:nt, :], gates[rr][0:nt, sl], ALU.mult, [pY, gates[rr]], [xo])
                        k.tt(dve, xo[0:nt, :], xo[0:nt, :], x1t[r][0:nt, :], ALU.add, [xo, x1t[r]], [xo])
                        if not last:
                            k.dma(xc2_d[tk:tk + nt, :], xo[0:nt, :], reads=[xo], writes=[xc2_d], q="pool")
                        else:
                            k.actf(junk[0:nt, :], xo[0:nt, :], AF.Square, [xo], [junk, ssq], accum_out=ssq[0:nt, m:m + 1])
                    if last:
                        if nv % 128:
                            pass
                        k.rstd(ssq[:, 8:8 + ngrp], ssq[:, 4:4 + ngrp], ssq[:, 0:ngrp], 1.0 / D, (ssq, ssq, ssq))
                        for m in range(ngrp):
                            nt = min(128, nv - 128 * m)
                            tk = tokb + 128 * m
                            xo = x2g[m]
                            k.stt(dve, xo[0:nt, :], xo[0:nt, :], ssq[0:nt, 8 + m:9 + m], fg[0:nt, :], ALU.mult, ALU.mult, [xo, ssq, fg], [xo])
                            k.dma(out_d[tk - C:tk - C + nt, :], xo[0:nt, :], reads=[xo], writes=[out_d], q="pool")
            k.barrier()
        def phase_lru():
            Y_d = scratch("Y_d", [10, 128, TOK], F32)
            GL_d = scratch("GL_d", [10, 128, T], BF16)
            with contextlib.ExitStack() as st:
                mod = load_mod(st, 1, 0, norm_mix)
                wl = k.sb(st, "wlin", [128, 8, 2560], BF16)
                with contextlib.ExitStack() as st2:
                    stg = [k.sb(st2, "wstg%d" % i, [128, 2560], F32) for i in range(2)]
                    cnt = [0]
                    for kc in range(8):
                        cast_load(stg, cnt, wl[:, kc, :], wl, lru_w_in[kc * 128:(kc + 1) * 128, :], lambda s_: s_[:, :])
                    k.barrier()
                junk = k.sb(st, "junk", [128, D], BF16)
                t32 = k.sb(st, "t32", [128, D], F32)
                hblk = [k.sb(st, "hblk%d" % i, [128, 8, 512], BF16) for i in range(2)]
                yb = [k.sb(st, "yb%d" % i, [128, 512], F32) for i in range(3)]
                gbl = [k.sb(st, "gbl%d" % i, [128, 512], BF16) for i in range(3)]
                psT = k.ps(st, "psTl", [128, 1024], BF16)
                psR = [k.ps(st, "psRp%d" % i, [128, 512]) for i in range(3)]
                psG = [k.ps(st, "psGp%d" % i, [128, 512]) for i in range(3)]
                blocks = [(0, 256)] + [(C + 512 * j, 512) for j in range(8)]
                ne = 0

                xts4 = [k.sb(st, "xq%d" % i, [128, D], F32) for i in range(4)]
                hb4 = [k.sb(st, "hq%d" % i, [128, D], BF16) for i in range(4)]
                ss4 = [k.sb(st, "sq%d" % i, [128, 4], F32) for i in range(4)]

                def fill_a(bi):
                    t0, n = blocks[bi]
                    for j in range(n // 128):
                        tok0 = t0 + j * 128
                        SG, SH, _ = mod[1 if tok0 < C else 0]
                        xt, sb_, hbj = xts4[j], ss4[j], hb4[j]
                        k.dma(xt[:], xc2_d[tok0:tok0 + 128, :], reads=[xc2_d], writes=[xt])
                        k.actf(junk[:], xt[:], AF.Square, [xt], [junk, sb_], accum_out=sb_[:, 0:1])
                        k.rstd(sb_[:, 2:3], sb_[:, 1:2], sb_[:, 0:1], 1.0 / D, (sb_, sb_, sb_))
                        k.stt(dve, t32[:], xt[:], sb_[:, 2:3], SG[:], ALU.mult, ALU.mult, [xt, sb_, SG], [t32])
                        k.tt(dve, hbj[:], t32[:], SH[:], ALU.add, [t32, SH], [hbj])

                def fill_b(bi, j):
                    t0, n = blocks[bi]
                    if j >= n // 128:
                        return
                    hk, hbj = hblk[bi % 2], hb4[j]
                    for kc in range(8):
                        k.tr(psT[:, kc * 128:(kc + 1) * 128], hbj[:, kc * 128:(kc + 1) * 128], identb[:], [hbj, identb], [psT])
                    k.cp(act, hk[:, :, j * 128:(j + 1) * 128], psT[:, 0:1024].rearrange("p (a b) -> p a b", a=8), [psT], [hk])

                fill_a(0)
                for j in range(4):
                    fill_b(0, j)
                for bi, (t0, n) in enumerate(blocks):
                    if bi + 1 < len(blocks):
                        fill_a(bi + 1)
                    hk = hblk[bi % 2]
                    for ch in range(10):
                        e = ne % 3
                        ne += 1
                        for kc in range(8):
                            k.mm(psR[e][:, 0:n], wl[:, kc, 1280 + ch * 128:1280 + (ch + 1) * 128], hk[:, kc, 0:n], kc == 0, kc == 7, [wl, hk], [psR[e]])
                        k.cp(act, yb[e][:, 0:n], psR[e][:, 0:n], [psR[e]], [yb[e]])
                        k.dma(Y_d[ch, :, t0:t0 + n], yb[e][:, 0:n], reads=[yb[e]], writes=[Y_d], q="pool")
                        if t0 >= C:
                            for kc in range(8):
                                k.mm(psG[e][:, 0:n], wl[:, kc, ch * 128:(ch + 1) * 128], hk[:, kc, 0:n], kc == 0, kc == 7, [wl, hk], [psG[e]])
                            k.actf(gbl[e][:, 0:n], psG[e][:, 0:n], AF.Gelu_apprx_tanh, [psG[e]], [gbl[e]])
                            k.dma(GL_d[ch, :, t0 - C:t0 - C + n], gbl[e][:, 0:n], reads=[gbl[e]], writes=[GL_d], q="pool")
                        if bi + 1 < len(blocks) and ch % 2 == 1 and ch // 2 < 4:
                            fill_b(bi + 1, ch // 2)
            k.barrier()
            with contextlib.ExitStack() as st:
                YW = TOK + 6
                Y = [k.sb(st, "Y%d" % i, [128, YW], F32) for i in range(2)]
                U = k.sb(st, "U", [128, TOK], F32)
                Ubs = [k.sb(st, "Ub%d" % i, [128, TOK], BF16) for i in range(2)]
                A = k.sb(st, "A", [128, TOK], F32)
                A1 = k.sb(st, "A1", [128, TOK], F32)
                X = k.sb(st, "X", [128, TOK], F32)
                Hf = k.sb(st, "Hf", [128, TOK], F32)
                RT = k.sb(st, "RT", [128, TOK], F32)
                IT = k.sb(st, "IT", [128, TOK], F32)
                GL = [k.sb(st, "GL%d" % i, [128, T], BF16) for i in range(2)]
                wg4 = [[k.sb(st, "wg4_%d_%d" % (j, i), [128, 128], BF16) for i in range(4)] for j in range(2)]
                stg = [k.sb(st, "lstg%d" % i, [128, 128], F32) for i in range(2)]
                cwl = k.sb(st, "cwl", [128, 10, 4], F32)
                lam = k.sb(st, "lam", [128, 2, 10], F32)
                sc8 = k.sb(st, "sc8", [128, 2, 10], F32)
                sc16 = k.sb(st, "sc16", [128, 2, 10], F32)
                ba = k.sb(st, "ba", [128, 2, 10], F32)
                bx = k.sb(st, "bx", [128, 2, 10], F32)
                psA = [k.ps(st, "psAl%d" % i, [128, 512]) for i in range(2)]
                psX = [k.ps(st, "psXl%d" % i, [128, 512]) for i in range(2)]
                k.dma(cwl[:], lru_conv[:, :, :], writes=[cwl])
                k.dma(lam[:], lru_lam[:, :, :], writes=[lam])
                k.dma(ba[:], lru_b_a[:, :, :], writes=[ba])
                k.dma(bx[:], lru_b_x[:, :, :], writes=[bx])
                k.actf(sc8[:], lam[:], AF.Exp, [lam], [sc8], scale=-1.0)
                k.actf(sc16[:], sc8[:], AF.Ln, [sc8], [sc16], bias=1.0)
                k.ts(dve, sc8[:], sc16[:], -8.0, None, ALU.mult, None, [sc16], [sc8])
                k.ts(dve, sc16[:], sc16[:], -16.0, None, ALU.mult, None, [sc16], [sc16])
                blocks = [(0, 256)] + [(C + 512 * j, 512) for j in range(8)]
                cnt = [0]
                nbc = [0]

                def pass1(ch):
                    w = ch % 2
                    Yc, GLc = Y[w], GL[w]
                    for d in range(2):
                        cast_load(stg, cnt, wg4[w][d][:], wg4[w][d], lru_w_a[d, ch, :, :], lambda s_: s_[:, :], engs=(pool, act))
                        cast_load(stg, cnt, wg4[w][2 + d][:], wg4[w][2 + d], lru_w_x[d, ch, :, :], lambda s_: s_[:, :], engs=(pool, act))
                    for (a_, b_) in ((0, 1), (257, 260), (YW - 2, YW)):
                        k.op(pool, lambda e, a_=a_, b_=b_: e.memset(Yc[:, a_:b_], 0.0), [], [Yc])
                    k.dma(Yc[:, 1:1 + C], Y_d[ch, :, 0:C], reads=[Y_d], writes=[Yc])
                    k.dma(Yc[:, 260:260 + T], Y_d[ch, :, C:TOK], reads=[Y_d], writes=[Yc])
                    k.dma(GLc[:], GL_d[ch, :, :], reads=[GL_d], writes=[GLc])

                def conv(ch):
                    Yc, Ubc = Y[ch % 2], Ubs[ch % 2]
                    for (c0, L, uo) in ((1, C, 0), (260, T, C)):
                        k.actf(U[:, uo:uo + L], Yc[:, c0 - 1:c0 - 1 + L], AF.Copy, [Yc, cwl], [U], scale=cwl[:, ch, 0:1])
                        for tap in (1, 2):
                            k.stt(dve, U[:, uo:uo + L], Yc[:, c0 - 1 + tap:c0 - 1 + tap + L], cwl[:, ch, tap:tap + 1], U[:, uo:uo + L],
                                  ALU.mult, ALU.add, [Yc, cwl, U], [U])
                        k.stt(dve, Ubc[:, uo:uo + L], Yc[:, c0 + 2:c0 + 2 + L], cwl[:, ch, 3:4], U[:, uo:uo + L],
                              ALU.mult, ALU.add, [Yc, cwl, U], [Ubc])

                pass1(0)
                conv(0)
                for ch in range(10):
                    w = ch % 2
                    Yc, GLc, Ub = Y[w], GL[w], Ubs[w]
                    if ch + 1 < 10:
                        pass1(ch + 1)
                    for d in range(2):
                        Ad = A if d == 0 else A1
                        Id = IT if d == 0 else X
                        for (t0, n) in blocks:
                            r = nbc[0] % 2
                            nbc[0] += 1
                            k.mm(psA[r][:, 0:n], wg4[w][d][:], Ub[:, t0:t0 + n], True, True, [wg4[w][d], Ub], [psA[r]])
                            k.mm(psX[r][:, 0:n], wg4[w][2 + d][:], Ub[:, t0:t0 + n], True, True, [wg4[w][2 + d], Ub], [psX[r]])
                            k.actf(RT[:, t0:t0 + n], psA[r][:, 0:n], AF.Sigmoid, [psA[r], ba], [RT], bias=ba[:, d, ch:ch + 1])
                            k.actf(Id[:, t0:t0 + n], psX[r][:, 0:n], AF.Sigmoid, [psX[r], bx], [Id], bias=bx[:, d, ch:ch + 1])
                        k.tt(pool, Id[:], Id[:], Ub[:], ALU.mult, [Id, Ub], [Id])
                        k.actf(Ad[:], RT[:], AF.Exp, [RT, sc8], [Ad], scale=sc8[:, d, ch:ch + 1])
                        k.actf(RT[:], RT[:], AF.Exp, [RT, sc16], [RT], scale=sc16[:, d, ch:ch + 1])
                        k.actf(RT[:], RT[:], AF.Sqrt, [RT], [RT], scale=-1.0, bias=1.0)
                        k.tt(dve, Id[:], RT[:], Id[:], ALU.mult, [RT, Id], [Id])
                        if d == 0:
                            k.op(dve, lambda e: e.tensor_tensor_scan(out=Hf[:, 0:C], data0=Ad[:, 0:C], data1=Id[:, 0:C], initial=0.0,
                                                                    op0=ALU.mult, op1=ALU.add), [Ad, Id], [Hf])
                            k.op(dve, lambda e: e.tensor_tensor_scan(out=Hf[:, C:TOK], data0=Ad[:, C:TOK], data1=Id[:, C:TOK],
                                                                    initial=Hf[:, C - 1:C], op0=ALU.mult, op1=ALU.add), [Ad, Id, Hf], [Hf])
                            if ch + 1 < 10:
                                conv(ch + 1)
                        else:
                            k.op(dve, lambda e: e.tensor_tensor_scan(out=Yc[:, 0:C][:, ::-1], data0=Ad[:, 0:C][:, ::-1], data1=Id[:, 0:C][:, ::-1],
                                                                    initial=0.0, op0=ALU.mult, op1=ALU.add), [Ad, Id], [Yc])
                            k.op(dve, lambda e: e.tensor_tensor_scan(out=Yc[:, C:TOK][:, ::-1], data0=Ad[:, C:TOK][:, ::-1],
                                                                    data1=Id[:, C:TOK][:, ::-1], initial=Yc[:, 0:1],
                                                                    op0=ALU.mult, op1=ALU.add), [Ad, Id, Yc], [Yc])
                    k.tt(dve, Hf[:, C:TOK], Hf[:, C:TOK], Yc[:, C:TOK], ALU.add, [Hf, Yc], [Hf])
                    k.tt(pool, GLc[:], Hf[:, C:TOK], GLc[:], ALU.mult, [Hf, GLc], [GLc])
                    k.dma(rgT_d[ch, :, :], GLc[:], reads=[GLc], writes=[rgT_d], q="pool")
            k.barrier()

        def phase_post_ffn(l):
            with contextlib.ExitStack() as ost:
                wup = k.sb(ost, "wupb", [128, 8, 2 * DFF], BF16)
                phase_post(l, wup)
                if STOP_AFTER == "post%d" % l:
                    return
                phase_ffn(l, wup)

        def phase_e1_att():
            with contextlib.ExitStack() as ost:
                qs = k.sb(ost, "qs", [128, 4, TOK], BF16)
                ks = k.sb(ost, "ks", [128, 2, TOK], BF16)
                vs = k.sb(ost, "vs", [128, NT, 130], BF16)
                phase_e1(qs, ks, vs)
                if STOP_AFTER == "e1":
                    return
                phase_att(qs, ks, vs)

        stages = [("p0", phase0), ("e1,att", phase_e1_att), ("gla", phase_gla),
                  ("post0,ffn0", lambda: phase_post_ffn(0)), ("lru", phase_lru), ("post1,ffn1", lambda: phase_post_ffn(1))]
        only = os.environ.get("MK_ONLY", "")
        for name, fn in stages:
            if only and not (set(name.split(",")) & set(only.split(","))):
                continue
            fn()
            if STOP_AFTER and STOP_AFTER in name.split(","):
                break
        k.barrier()
    return nc, dbg


def _host_inputs(inp, b):
    f = np.float32
    m = {}
    m["xc"] = np.ascontiguousarray(np.concatenate([inp["ctx"][b], inp["x"][b]], axis=0), dtype=f)
    c2 = np.stack([inp["c"][b], inp["c_ctx"]], axis=1)
    m["c2T"] = np.ascontiguousarray(c2.reshape(8, 128, 2).transpose(1, 0, 2), dtype=f)
    for nme in ("ada_w", "ada_b", "norm_mix", "norm_ffn", "ffn_w_up", "ffn_w_down"):
        m[nme] = np.ascontiguousarray(inp[nme], dtype=f)
    m["ffn_conv"] = np.ascontiguousarray(inp["ffn_conv"].reshape(2, 3, 44, 128).transpose(0, 3, 2, 1), dtype=f)
    m["even_w_in"] = np.ascontiguousarray(inp["even_w_in"][0], dtype=f)
    m["even_w_out"] = np.ascontiguousarray(inp["even_w_out"][0], dtype=f)
    m["qk_gain"] = np.ascontiguousarray(np.concatenate([np.tile(inp["attn_q_gain"][0], 8), np.tile(inp["attn_k_gain"][0], 2)])[None, :], dtype=f)
    gw = np.zeros((33, 512), f)
    gw[0:16, 0:256] = inp["gla_gate_w_up"][0, 0]
    gw[16:32, 256:512] = inp["gla_gate_w_up"][0, 1]
    gw[32, 0:256] = inp["gla_gate_b"][0, 0]
    gw[32, 256:512] = inp["gla_gate_b"][0, 1]
    m["gate_wup"] = gw
    m["gla_og"] = np.ascontiguousarray(np.tile(inp["gla_out_gain"][0], 4)[None, :], dtype=f)
    m["lru_w_in"] = np.ascontiguousarray(inp["lru_w_in"][0], dtype=f)
    m["lru_conv"] = np.ascontiguousarray(inp["lru_conv"][0].reshape(4, 10, 128).transpose(2, 1, 0), dtype=f)
    for nme, src in (("lru_lam", "lru_lambda"), ("lru_b_a", "lru_b_a"), ("lru_b_x", "lru_b_x")):
        m[nme] = np.ascontiguousarray(inp[src][0].reshape(2, 10, 128).transpose(2, 0, 1), dtype=f)
    m["lru_w_a"] = np.ascontiguousarray(inp["lru_w_a"][0], dtype=f)
    m["lru_w_x"] = np.ascontiguousarray(inp["lru_w_x"][0], dtype=f)
    m["lru_w_out"] = np.ascontiguousarray(inp["lru_w_out"][0], dtype=f)
    m["final_gain"] = np.ascontiguousarray(inp["final_gain"][None, :], dtype=f)
    return m


def _consts():
    f = np.float32
    s = np.arange(128)[:, None]
    t = np.arange(128)[None, :]
    c = {"c_ident": np.eye(128, dtype=f), "c_maskf": (s <= t).astype(f), "c_maskb": (s >= t).astype(f)}
    tok = (np.arange(32)[None, :] * 128 + np.arange(128)[:, None]).astype(np.int64)
    row = (tok // 64).astype(f)
    col = (tok % 64).astype(f)
    inv = (np.float32(10000.0) ** (-np.arange(16, dtype=f) / np.float32(16))).astype(f)
    ang = np.concatenate([row[:, :, None] * inv[None, None, :], col[:, :, None] * inv[None, None, :]], axis=-1).astype(f)
    cs, sn = np.cos(ang), np.sin(ang)
    cosf = np.concatenate([cs[..., 0:16], cs[..., 0:16], cs[..., 16:32], cs[..., 16:32]], axis=-1)
    sinf = np.concatenate([-sn[..., 0:16], sn[..., 0:16], -sn[..., 16:32], sn[..., 16:32]], axis=-1)
    c["c_cos"] = np.ascontiguousarray(cosf, dtype=f)
    c["c_sin"] = np.ascontiguousarray(sinf, dtype=f)
    return c


_CACHE = {}


def kernel(**inputs):
    inp = {k_: np.asarray(v) for k_, v in inputs.items()}
    if "nc" not in _CACHE:
        _CACHE["nc"] = build_program()
    nc, dbg = _CACHE["nc"]
    consts = _consts()
    ncores = int(os.environ.get("MK_CORES", "8"))
    in_maps = []
    for b in range(ncores):
        m = _host_inputs(inp, b)
        m.update(consts)
        in_maps.append(m)
    res = run_bass_kernel_spmd(nc, in_maps, core_ids=list(range(ncores)))
    if DEBUG:
        _CACHE["res"] = res
    out = np.stack([np.asarray(r["out"], dtype=np.float32) for r in res.results], axis=0)
    return out
```
